# Optimizing a Trainium2 kernel written in Bass

```python
import jax, jax.numpy as jnp
from jax import lax
import numpy as np

D_MODEL = 1024
BATCH = 32
SEQ = 2048
DEPTH = 4
DEC_BATCH = 32
DEC_SEQ = 32
PAST_LEN = 1024

CHUNK = 64
PAST_CHUNKS = 8
BAND_PAST = PAST_CHUNKS * CHUNK
BAND = BAND_PAST + CHUNK
HEAD_DIM = 64
N_HEADS = D_MODEL // HEAD_DIM
REL_CLIP = 128
CONV_W = 31
D_FF = ((-(-8 * D_MODEL // 3)) + 255) // 256 * 256
N_ATTN = (DEPTH + 1) // 2
N_CONV = DEPTH // 2
EPS = 1e-6
NEG_INF = -1e30

kernel_name = "streaming_attn_conformer_hybrid_step"


def rms_norm(x, g):
    xf = x.astype(jnp.float32)
    y = xf * lax.rsqrt(jnp.mean(xf * xf, axis=-1, keepdims=True) + EPS)
    return (y * g.astype(jnp.float32)).astype(x.dtype)


def layer_norm(x, g, b):
    xf = x.astype(jnp.float32)
    xc = xf - jnp.mean(xf, axis=-1, keepdims=True)
    y = xc * lax.rsqrt(jnp.mean(xc * xc, axis=-1, keepdims=True) + EPS)
    return (y * g.astype(jnp.float32) + b.astype(jnp.float32)).astype(x.dtype)


def rel_bias(table, q_pos, k_pos):
    idx = jnp.clip(q_pos[:, None] - k_pos[None, :], -REL_CLIP, REL_CLIP) + REL_CLIP
    return table.astype(jnp.float32)[:, idx]


def attend(q, k, v, bias, valid):
    s = jnp.einsum('bqhd,bkhd->bhqk', q, k).astype(jnp.float32) * (HEAD_DIM ** -0.5) + bias[None]
    s = jnp.where(valid[None, None, None, :], s, NEG_INF)
    p = jax.nn.softmax(s, axis=-1).astype(v.dtype)
    return jnp.einsum('bhqk,bkhd->bqhd', p, v)


def qkv_proj(h, w_qkv, g_q, g_k):
    b, l, _ = h.shape
    qkv = (h @ w_qkv).reshape(b, l, 3, N_HEADS, HEAD_DIM)
    q = rms_norm(qkv[:, :, 0], g_q)
    k = rms_norm(qkv[:, :, 1], g_k)
    return q, k, qkv[:, :, 2]


def attn_prompt(h, w_qkv, g_q, g_k, w_o, table):
    b, l, _ = h.shape
    q, k, v = qkv_proj(h, w_qkv, g_q, g_k)
    pad = ((0, 0), (BAND_PAST, 0), (0, 0), (0, 0))
    k_pad = jnp.pad(k, pad)
    v_pad = jnp.pad(v, pad)
    bias = rel_bias(table, BAND_PAST + jnp.arange(CHUNK), jnp.arange(BAND))

    def one_chunk(c):
        start = c * CHUNK
        q_c = lax.dynamic_slice_in_dim(q, start, CHUNK, axis=1)
        k_c = lax.dynamic_slice_in_dim(k_pad, start, BAND, axis=1)
        v_c = lax.dynamic_slice_in_dim(v_pad, start, BAND, axis=1)
        valid = (start - BAND_PAST + jnp.arange(BAND)) >= 0
        return attend(q_c, k_c, v_c, bias, valid)

    o = lax.map(one_chunk, jnp.arange(l // CHUNK))
    o = jnp.moveaxis(o, 0, 1).reshape(b, l, D_MODEL)
    keep = min(BAND_PAST, l)
    return o @ w_o, k[:, l - keep:], v[:, l - keep:]


def attn_sample(h, ck, cv, w_qkv, g_q, g_k, w_o, table):
    b, l, _ = h.shape
    q, k, v = qkv_proj(h, w_qkv, g_q, g_k)
    n_past = ck.shape[1]
    k_all = jnp.concatenate([ck.astype(k.dtype), k], axis=1)
    v_all = jnp.concatenate([cv.astype(v.dtype), v], axis=1)
    q_pos = PAST_LEN + jnp.arange(l)
    k_pos = jnp.concatenate([PAST_LEN - n_past + jnp.arange(n_past), q_pos])
    bias = rel_bias(table, q_pos, k_pos)
    o = attend(q, k_all, v_all, bias, k_pos >= 0).reshape(b, l, D_MODEL)
    return o @ w_o, k, v


def conv_pre(h, pw1_w, pw1_b):
    a = h @ pw1_w + pw1_b
    return a[..., :D_MODEL] * jax.nn.sigmoid(a[..., D_MODEL:])


def conv_post(ctx, dw_w, dw_b, ln_g, ln_b, pw2_w, pw2_b):
    y = lax.conv_general_dilated(ctx, dw_w[:, None, :].astype(ctx.dtype), (1,), 'VALID',
                                 dimension_numbers=('NWC', 'WIO', 'NWC'),
                                 feature_group_count=D_MODEL) + dw_b
    y = jax.nn.silu(layer_norm(y, ln_g, ln_b))
    return y @ pw2_w + pw2_b


def swiglu(h, w_in, w_out):
    gu = h @ w_in
    return (jax.nn.silu(gu[..., :D_FF]) * gu[..., D_FF:]) @ w_out


def setup_inputs(seed: int = 0) -> dict:
    key = jax.random.key(seed)
    ks = jax.random.split(key, 24)

    def nrm(k, shape, scale):
        return jax.random.normal(k, shape, jnp.float32) * scale

    kv_len = min(BAND_PAST, PAST_LEN)
    return {
        "x_prompt": nrm(ks[0], (BATCH, SEQ, D_MODEL), 1.0),
        "x_sample": nrm(ks[1], (DEC_BATCH, DEC_SEQ, D_MODEL), 1.0),
        "cache_k": nrm(ks[2], (N_ATTN, DEC_BATCH, kv_len, N_HEADS, HEAD_DIM), 1.0),
        "cache_v": nrm(ks[3], (N_ATTN, DEC_BATCH, kv_len, N_HEADS, HEAD_DIM), 1.0),
        "state_conv": nrm(ks[4], (N_CONV, DEC_BATCH, CONV_W - 1, D_MODEL), 0.5),
        "norm_mix": 1.0 + nrm(ks[5], (DEPTH, D_MODEL), 0.01),
        "norm_ffn": 1.0 + nrm(ks[6], (DEPTH, D_MODEL), 0.01),
        "w_qkv": nrm(ks[7], (N_ATTN, D_MODEL, 3 * D_MODEL), D_MODEL ** -0.5),
        "q_norm": 1.0 + nrm(ks[8], (N_ATTN, HEAD_DIM), 0.01),
        "k_norm": 1.0 + nrm(ks[9], (N_ATTN, HEAD_DIM), 0.01),
        "rel_table": nrm(ks[10], (N_ATTN, N_HEADS, 2 * REL_CLIP + 1), 0.1),
        "w_o": nrm(ks[11], (N_ATTN, D_MODEL, D_MODEL), D_MODEL ** -0.5),
        "pw1_w": nrm(ks[12], (N_CONV, D_MODEL, 2 * D_MODEL), D_MODEL ** -0.5),
        "pw1_b": nrm(ks[13], (N_CONV, 2 * D_MODEL), 0.01),
        "dw_w": nrm(ks[14], (N_CONV, CONV_W, D_MODEL), CONV_W ** -0.5),
        "dw_b": nrm(ks[15], (N_CONV, D_MODEL), 0.01),
        "conv_ln_g": 1.0 + nrm(ks[16], (N_CONV, D_MODEL), 0.01),
        "conv_ln_b": nrm(ks[17], (N_CONV, D_MODEL), 0.01),
        "pw2_w": nrm(ks[18], (N_CONV, D_MODEL, D_MODEL), D_MODEL ** -0.5),
        "pw2_b": nrm(ks[19], (N_CONV, D_MODEL), 0.01),
        "ffn_w_in": nrm(ks[20], (DEPTH, D_MODEL, 2 * D_FF), D_MODEL ** -0.5),
        "ffn_w_out": nrm(ks[21], (DEPTH, D_FF, D_MODEL), D_FF ** -0.5),
    }


def reference(x_prompt, x_sample, cache_k, cache_v, state_conv, norm_mix, norm_ffn,
              w_qkv, q_norm, k_norm, rel_table, w_o, pw1_w, pw1_b, dw_w, dw_b,
              conv_ln_g, conv_ln_b, pw2_w, pw2_b, ffn_w_in, ffn_w_out):
    xp, xs = x_prompt, x_sample
    kp_new, vp_new, ks_new, vs_new, cp_new, cs_new = [], [], [], [], [], []
    for layer in range(DEPTH):
        hp = rms_norm(xp, norm_mix[layer])
        hs = rms_norm(xs, norm_mix[layer])
        if layer % 2 == 0:
            a = layer // 2
            mp, kp, vp = attn_prompt(hp, w_qkv[a], q_norm[a], k_norm[a], w_o[a], rel_table[a])
            ms, kss, vss = attn_sample(hs, cache_k[a], cache_v[a], w_qkv[a], q_norm[a],
                                       k_norm[a], w_o[a], rel_table[a])
            kp_new.append(kp)
            vp_new.append(vp)
            ks_new.append(kss)
            vs_new.append(vss)
        else:
            c = layer // 2
            up = conv_pre(hp, pw1_w[c], pw1_b[c])
            us = conv_pre(hs, pw1_w[c], pw1_b[c])
            ctx_p = jnp.pad(up, ((0, 0), (CONV_W - 1, 0), (0, 0)))
            ctx_s = jnp.concatenate([state_conv[c].astype(us.dtype), us], axis=1)
            mp = conv_post(ctx_p, dw_w[c], dw_b[c], conv_ln_g[c], conv_ln_b[c], pw2_w[c], pw2_b[c])
            ms = conv_post(ctx_s, dw_w[c], dw_b[c], conv_ln_g[c], conv_ln_b[c], pw2_w[c], pw2_b[c])
            cp_new.append(up[:, -(CONV_W - 1):])
            cs_new.append(ctx_s[:, -(CONV_W - 1):])
        xp = xp + mp
        xs = xs + ms
        xp = xp + swiglu(rms_norm(xp, norm_ffn[layer]), ffn_w_in[layer], ffn_w_out[layer])
        xs = xs + swiglu(rms_norm(xs, norm_ffn[layer]), ffn_w_in[layer], ffn_w_out[layer])
    return (xp, xs, jnp.stack(kp_new), jnp.stack(vp_new), jnp.stack(ks_new),
            jnp.stack(vs_new), jnp.stack(cp_new), jnp.stack(cs_new))
```

```python
import contextlib
import numpy as np
import ml_dtypes
import concourse.bass as bass
import concourse.mybir as mybir
from concourse.bass_utils import run_bass_kernel_spmd

F32 = mybir.dt.float32
BF16 = mybir.dt.bfloat16
AF = mybir.ActivationFunctionType
ALU = mybir.AluOpType

D = 1024
NCH = 8
DFF = 2816
NFF = 22
NH = 16
CW = 31
EPS = 1e-6
NEG = -30000.0
ENGS = ("pe", "act", "dve", "pool", "sp")
NDMASEM = 48
NCAST = 16
NPOOLQ = 8
NXIO = 8
NSLOT = 4

V_NMIX = 0
V_NFFN = 32
V_PW1B = 64
V_DWB = 96
V_LNG = 112
V_LNB = 128
V_PW2B = 144
V_GQ = 160
V_GK = 162
V_DWW = 164
NV = V_DWW + 2 * 8 * CW


class Sched:
    def __init__(self):
        self.ops = {e: [] for e in ENGS}
        self.cnt = {e: 0 for e in ENGS}
        self.lastw = {}
        self.readers = {}
        self.waited = {e: {} for e in ENGS}
        self.dcnt = [0] * NDMASEM
        self.drr = {"a": 0, "b": 0, "c": 0, "e": 0}

    def _deps(self, eng, reads, writes):
        deps = {}

        def add(tok):
            if tok is None:
                return
            s, v = tok
            if deps.get(s, 0) < v:
                deps[s] = v

        for k in reads:
            add(self.lastw.get(k))
            if isinstance(k, tuple) and k[0] == "ps":
                rd = self.readers.get(k)
                if rd:
                    for s, v in rd.items():
                        if s != eng:
                            add((s, v))
        for k in writes:
            add(self.lastw.get(k))
            rd = self.readers.get(k)
            if rd:
                for s, v in rd.items():
                    add((s, v))
        waits = []
        wd = self.waited[eng]
        for s, v in deps.items():
            if s == eng and eng == "pe":
                continue
            if wd.get(s, 0) >= v:
                continue
            wd[s] = v
            waits.append((s, v))
        return waits

    def _commit(self, tok, reads, writes):
        s, v = tok
        for k in reads:
            rd = self.readers.setdefault(k, {})
            if rd.get(s, 0) < v:
                rd[s] = v
        for k in writes:
            self.lastw[k] = tok
            self.readers[k] = {}

    def op(self, eng, fn, reads=(), writes=(), inc=True):
        waits = self._deps(eng, reads, writes)
        if inc:
            self.cnt[eng] += 1
            tok = (eng, self.cnt[eng])
        else:
            tok = (eng, self.cnt[eng] + 1)
        self.ops[eng].append((waits, fn, eng if inc else None, 1))
        self._commit(tok, reads, writes)
        return tok

    def barrier(self):
        toks = [(e, self.cnt[e]) for e in ("pe", "act", "dve", "pool") if self.cnt[e] > 0]
        toks += [(("d", i), self.dcnt[i]) for i in range(NCAST, NDMASEM)
                 if self.dcnt[i] > 0 and not (NCAST + NPOOLQ <= i < NCAST + NPOOLQ + NXIO)]
        for e in ENGS:
            waits = []
            wd = self.waited[e]
            for sm, v in toks:
                if sm == e or wd.get(sm, 0) >= v:
                    continue
                wd[sm] = v
                waits.append((sm, v))
            if waits:
                self.ops[e].append((waits, None, None, 0))

    def dma(self, eng, fn, reads=(), writes=(), pool="b"):
        waits = self._deps(eng, reads, writes)
        if eng == "pool" and pool == "b":
            pool = "c"
        if pool == "a":
            i = self.drr["a"] % NCAST
        elif pool == "c":
            i = NCAST + self.drr["c"] % NPOOLQ
        elif pool == "e":
            i = NCAST + NPOOLQ + self.drr["e"] % NXIO
        else:
            i = NCAST + NPOOLQ + NXIO + self.drr["b"] % (NDMASEM - NCAST - NPOOLQ - NXIO)
        self.drr[pool] += 1
        s = ("d", i)
        wd = self.waited[eng]
        if self.dcnt[i] > 0 and wd.get(s, 0) < self.dcnt[i]:
            wd[s] = self.dcnt[i]
            waits.append((s, self.dcnt[i]))
        self.dcnt[i] += 16
        tok = (s, self.dcnt[i])
        self.ops[eng].append((waits, fn, s, 16))
        self._commit(tok, reads, writes)
        return tok


class Gen:
    def __init__(self, nc, cfg):
        self.nc = nc
        self.cfg = cfg
        self.S = Sched()
        self.psrr = {"mm": 0, "acc": 0, "all": 0}
        self.projpool = "all"
        self.rr = {}

    def bank(self, pool):
        if pool == "mm":
            b = self.psrr["mm"] % 4
        elif pool == "acc":
            b = 4 + self.psrr["acc"] % 4
        else:
            b = self.psrr["all"] % 8
        self.psrr[pool] += 1
        return b

    def rot(self, name, n):
        i = self.rr.get(name, 0)
        self.rr[name] = i + 1
        return i % n

    def mm(self, out, lhsT, rhs, start, stop, reads, writes, inc):
        self.S.op("pe", lambda e, out=out, lhsT=lhsT, rhs=rhs, start=start, stop=stop:
                  e.matmul(out, lhsT, rhs, start=start, stop=stop, skip_group_check=True),
                  reads=reads, writes=writes, inc=inc)

    def tr(self, out, in_, ident, reads, writes, inc):
        self.S.op("pe", lambda e, out=out, in_=in_, ident=ident: e.transpose(out, in_, ident),
                  reads=reads, writes=writes, inc=inc)

    def act(self, out, in_, func, reads, writes, bias=None, scale=None):
        kw = {}
        if bias is not None:
            kw["bias"] = bias
        if scale is not None:
            kw["scale"] = scale
        self.S.op("act", lambda e, out=out, in_=in_, func=func, kw=kw: e.activation(out, in_, func, **kw),
                  reads=reads, writes=writes)

    def tt(self, eng, out, in0, in1, op, reads, writes):
        self.S.op(eng, lambda e, out=out, in0=in0, in1=in1, op=op: e.tensor_tensor(out, in0, in1, op),
                  reads=reads, writes=writes)

    def ts(self, eng, out, in0, s1, s2, op0, op1, reads, writes):
        if op1 is None:
            self.S.op(eng, lambda e, out=out, in0=in0, s1=s1, op0=op0: e.tensor_scalar(out, in0, s1, None, op0),
                      reads=reads, writes=writes)
        else:
            self.S.op(eng, lambda e, out=out, in0=in0, s1=s1, s2=s2, op0=op0, op1=op1:
                      e.tensor_scalar(out, in0, s1, s2, op0, op1), reads=reads, writes=writes)

    def stt(self, eng, out, in0, sc, in1, op0, op1, reads, writes):
        self.S.op(eng, lambda e, out=out, in0=in0, sc=sc, in1=in1, op0=op0, op1=op1:
                  e.scalar_tensor_tensor(out, in0, sc, in1, op0, op1), reads=reads, writes=writes)

    def rsqrt(self, out, in_, reads, wkey):
        self.act(out, in_, AF.Ln, reads=list(reads) + ["const"], writes=[wkey], bias=self.epsc[:, 0:1])
        self.act(out, out, AF.Exp, reads=[wkey], writes=[wkey], scale=-0.5)

    def cp(self, eng, out, in_, reads, writes):
        self.S.op(eng, lambda e, out=out, in_=in_: e.tensor_copy(out, in_), reads=reads, writes=writes)

    def memset(self, eng, ap, val, writes):
        self.S.op(eng, lambda e, ap=ap, val=val: e.memset(ap, val), writes=writes)

    def dma(self, eng, out, in_, reads, writes, pool="b"):
        return self.S.dma(eng, lambda e, out=out, in_=in_: e.dma_start(out=out, in_=in_), reads=reads, writes=writes,
                          pool=pool)

    def plan_weights(self):
        self.tid = {}
        n = 0
        for L in range(self.cfg["DEPTH"]):
            if L % 2 == 0:
                names = ["q0", "q1", "k0", "k1", "v0", "v1", "o0", "o1"]
            else:
                names = ["p0", "p1", "p2", "p3", "w0", "w1"]
            names += ["i%d" % i for i in range(11)] + ["f%d" % i for i in range(8)]
            for nm in names:
                self.tid[(L, nm)] = n
                n += 1
        self.ntiles = n

    def emit_casts(self):
        g = self
        W = self.W
        wscr = self.wscr

        def cast(tid, dst, src):
            g.dma("pool", dst, src, reads=(), writes=[("wscr", tid)], pool="a")

        for L in range(self.cfg["DEPTH"]):
            a = L // 2
            if L % 2 == 0:
                for i, nm in enumerate(["q0", "q1", "k0", "k1", "v0", "v1"]):
                    tid = self.tid[(L, nm)]
                    src = W["w_qkv"][a, :, i * 512:(i + 1) * 512].rearrange("(kc p) n -> p kc n", p=128)
                    cast(tid, wscr[tid].rearrange("p (kc n) -> p kc n", kc=8), src)
                for i, nm in enumerate(["o0", "o1"]):
                    tid = self.tid[(L, nm)]
                    src = W["w_o"][a, :, i * 512:(i + 1) * 512].rearrange("(kc p) n -> p kc n", p=128)
                    cast(tid, wscr[tid].rearrange("p (kc n) -> p kc n", kc=8), src)
            else:
                for pi in range(4):
                    tid = self.tid[(L, "p%d" % pi)]
                    dst4 = wscr[tid].rearrange("p (kc q n) -> p kc q n", kc=8, q=4)
                    for jj in range(2):
                        j = 2 * pi + jj
                        for gsel in range(2):
                            c0 = gsel * D + j * 128
                            src = W["pw1_w"][a, :, c0:c0 + 128].rearrange("(kc p) n -> p kc n", p=128)
                            cast(tid, dst4[:, :, 2 * jj + gsel, :], src)
                for i, nm in enumerate(["w0", "w1"]):
                    tid = self.tid[(L, nm)]
                    src = W["pw2_w"][a, :, i * 512:(i + 1) * 512].rearrange("(kc p) n -> p kc n", p=128)
                    cast(tid, wscr[tid].rearrange("p (kc n) -> p kc n", kc=8), src)
            for ii in range(11):
                tid = self.tid[(L, "i%d" % ii)]
                dst4 = wscr[tid].rearrange("p (kc q n) -> p kc q n", kc=8, q=4)
                for jj in range(2):
                    j = 2 * ii + jj
                    for gsel in range(2):
                        c0 = gsel * DFF + j * 128
                        src = W["ffn_w_in"][L, :, c0:c0 + 128].rearrange("(kc p) n -> p kc n", p=128)
                        cast(tid, dst4[:, :, 2 * jj + gsel, :], src)
            for o in range(8):
                tid = self.tid[(L, "f%d" % o)]
                src = W["ffn_w_out"][L, :, o * 128:(o + 1) * 128].rearrange("(kc p) n -> p kc n", p=128)
                cast(tid, wscr[tid][:, 0:NFF * 128].rearrange("p (kc n) -> p kc n", kc=NFF), src)

    def wnext(self, L, nm):
        i = self.wpos
        assert self.wseq[i] == (L, nm), (i, self.wseq[i], (L, nm))
        self.wpos += 1
        while self.wissued < min(len(self.wseq), i + NSLOT - 1):
            k = self.wissued
            tid = self.tid[self.wseq[k]]
            s = k % NSLOT
            ncol = NFF * 128 if self.wseq[k][1].startswith("f") else 4096
            self.dma("sp", self.wslots[:, s, 0:ncol], self.wscr[tid][:, 0:ncol], reads=[("wscr", tid)], writes=[("ws", s)])
            self.wissued += 1
        s = i % NSLOT
        return self.wslots[:, s, :], ("ws", s)

    def rmsnorm(self, xap, xkeys, gbase, W, hdst, hkeys, tcols, sq, sqkeys, hextra=()):
        self.act(sq[:, :, 0:W], xap, AF.Square, reads=xkeys, writes=sqkeys)
        b = self.bank("mm")
        for j in range(NCH):
            self.mm(self.ps[:, b, 0:W], self.meanmat[:], sq[:, j, 0:W], j == 0, j == NCH - 1,
                    reads=sqkeys + ["const"], writes=[("ps", b)], inc=(j == NCH - 1))
        ri = 0
        rstd = self.rstd[:, 0:W]
        self.rsqrt(rstd, self.ps[:, b, 0:W], reads=[("ps", b)], wkey=("rstd", ri))
        for j in range(NCH):
            eng = "dve"
            self.stt(eng, hdst[:, j, tcols], xap[:, j, :], self.vecs[:, gbase + j:gbase + j + 1], rstd,
                     ALU.mult, ALU.mult, reads=[xkeys[j], ("rstd", ri), "const"], writes=[hkeys[j]] + list(hextra))

    def proj_feat(self, wslot_ap, wkey, csel, src, srckeys, W, nk=NCH):
        b = self.bank(self.projpool)
        for kc in range(nk):
            self.mm(self.ps[:, b, 0:W], csel(kc), src(kc), kc == 0, kc == nk - 1,
                    reads=[wkey] + srckeys, writes=[("ps", b)], inc=(kc == nk - 1))
        return b

    def load_x_seq(self, s, tts=None):
        for tt in (range(self.cfg["SEQ"] // 128) if tts is None else tts):
            xi = self.rot("xin", 2)
            self.dma("sp", self.xin[:, xi, :], self.xp[s, tt * 128:(tt + 1) * 128, :], reads=[], writes=[("xin", xi)],
                     pool="e")
            for g4 in range(2):
                b = self.bank("mm")
                for q in range(4):
                    j = g4 * 4 + q
                    self.tr(self.ps[:, b, q * 128:(q + 1) * 128], self.xin[:, xi, j * 128:(j + 1) * 128], self.identf[:],
                            reads=[("xin", xi), "const"], writes=[("ps", b)], inc=(q == 3))
                t = tt // 4
                eng = "dve" if g4 == 0 else "act"
                out = self.xT[:, g4 * 4:(g4 + 1) * 4, tt * 128:(tt + 1) * 128]
                in_ = self.ps[:, b, :].rearrange("p (q n) -> p q n", q=4)
                wk = [("x", j, t) for j in range(g4 * 4, g4 * 4 + 4)]
                if eng == "dve":
                    self.cp("dve", out, in_, reads=[("ps", b)], writes=wk)
                else:
                    self.act(out, in_, AF.Copy, reads=[("ps", b)], writes=wk)

    def store_y_seq(self, s, tts=None):
        for tt in (range(self.cfg["SEQ"] // 128) if tts is None else tts):
            t = tt // 4
            for g4 in range(2):
                b = self.bank("mm")
                for q in range(4):
                    j = g4 * 4 + q
                    self.tr(self.ps[:, b, q * 128:(q + 1) * 128], self.xT[:, j, tt * 128:(tt + 1) * 128], self.identf[:],
                            reads=[("x", j, t), "const"], writes=[("ps", b)], inc=(q == 3))
                oi = self.rot("ost", 3)
                eng = "dve" if g4 == 0 else "act"
                if eng == "dve":
                    self.cp("dve", self.ost[:, oi, :], self.ps[:, b, :], reads=[("ps", b)], writes=[("ost", oi)])
                else:
                    self.act(self.ost[:, oi, :], self.ps[:, b, :], AF.Copy, reads=[("ps", b)], writes=[("ost", oi)])
                self.dma("sp", self.yp[s, tt * 128:(tt + 1) * 128, g4 * 512:(g4 + 1) * 512], self.ost[:, oi, :],
                         reads=[("ost", oi)], writes=[], pool="e")

    def qkv_tile(self, L, s, t, sample=False):
        a = L // 2
        W = 128 if sample else 512
        h = self.h
        hk = [("h", j) for j in range(NCH)]
        last = (not sample) and (t == self.cfg["SEQ"] // 512 - 1) and "nokout" not in self.cfg.get("SKIP", ())
        for which in ("q", "k"):
            for c in range(NCH):
                if c % 4 == 0:
                    wsl, wkey = self.wnext(L, "%s%d" % (which, c // 4))
                    w3 = wsl.rearrange("p (kc n) -> p kc n", kc=8)
                cc = c % 4
                b = self.proj_feat(wsl, wkey, lambda kc, w3=w3, cc=cc: w3[:, kc, cc * 128:(cc + 1) * 128],
                                   lambda kc: h[:, kc, 0:W], hk, W)
                si = self.rot("sqh", 2)
                self.act(self.sqh[:, si, 0:W], self.ps[:, b, 0:W], AF.Square, reads=[("ps", b)], writes=[("sqh", si)])
                b2 = self.bank(self.projpool)
                self.mm(self.ps[:, b2, 0:W], self.blkmean[:], self.sqh[:, si, 0:W], True, True,
                        reads=[("sqh", si), "const"], writes=[("ps", b2)], inc=True)
                ri = self.rot("rs", 2)
                self.rsqrt(self.rs[:, ri, 0:W], self.ps[:, b2, 0:W], reads=[("ps", b2)], wkey=("rs", ri))
                if which == "q":
                    for par in range(2):
                        p0 = 64 * par
                        self.stt("dve", self.qT[p0:p0 + 64, c, par, 0:W], self.ps[p0:p0 + 64, b, 0:W],
                                 self.gq8[p0:p0 + 64, a:a + 1], self.rs[p0:p0 + 64, ri, 0:W], ALU.mult, ALU.mult,
                                 reads=[("ps", b), ("rs", ri), "const"], writes=[("q", c)])
                else:
                    gcol = self.vecs[:, V_GK + a:V_GK + a + 1]
                    if sample:
                        kdst = self.kTs[:, c, 0:W]
                        kkey = ("kTs", c)
                    else:
                        ks = (t % 2) * 512
                        kdst = self.kT[:, c, ks:ks + 512]
                        kkey = ("kT", c, t % 2)
                    if last or sample:
                        fi = self.rot("kf", 1)
                        kf = self.kf[:, fi, 0:W]
                        self.stt("dve", kf, self.ps[:, b, 0:W], gcol, self.rs[:, ri, 0:W], ALU.mult, ALU.mult,
                                 reads=[("ps", b), ("rs", ri), "const"], writes=[("kf", fi)])
                        self.act(kdst, kf, AF.Copy, reads=[("kf", fi)], writes=[kkey])
                        nsub = W // 128 if "nokt" not in self.cfg.get("SKIP", ()) else 0
                        b3 = self.bank(self.projpool)
                        for sub in range(nsub):
                            self.tr(self.ps[:, b3, sub * 128:(sub + 1) * 128], self.kf[:, fi, sub * 128:(sub + 1) * 128],
                                    self.identf[:], reads=[("kf", fi), "const"], writes=[("ps", b3)], inc=(sub == nsub - 1))
                        oi = self.rot("ost", 3)
                        if nsub:
                            self.act(self.ost[:, oi, 0:W], self.ps[:, b3, 0:W], AF.Copy, reads=[("ps", b3)], writes=[("ost", oi)])
                        if "nokdma" in self.cfg.get("SKIP", ()):
                            pass
                        elif sample:
                            dst = self.ksn[a, :, c * 128:(c + 1) * 128]
                            self.dma("sp", dst, self.ost[:, oi, 0:128], reads=[("ost", oi)], writes=[])
                        else:
                            dst = self.kp[a, s, :, c * 128:(c + 1) * 128].rearrange("(sub p) n -> p sub n", p=128)
                            self.dma("sp", dst, self.ost[:, oi, :].rearrange("p (sub n) -> p sub n", sub=4),
                                     reads=[("ost", oi)], writes=[])
                    else:
                        self.stt("dve", kdst, self.ps[:, b, 0:W], gcol, self.rs[:, ri, 0:W], ALU.mult, ALU.mult,
                                 reads=[("ps", b), ("rs", ri), "const"], writes=[kkey])
        wv = []
        for hv in range(2):
            wsl, wkey = self.wnext(L, "v%d" % hv)
            wv.append((wsl.rearrange("p (kc n) -> p kc n", kc=8), wkey))
        nsub = 4
        TP = 32 if sample else 128
        for sub in range(nsub):
            for hv in range(2):
                w3, wkey = wv[hv]
                b = self.bank("mm")
                for kc in range(NCH):
                    self.mm(self.ps[0:TP, b, :], h[:, kc, sub * TP:(sub + 1) * TP], w3[:, kc, :], kc == 0, kc == NCH - 1,
                            reads=[wkey] + hk, writes=[("ps", b)], inc=(kc == NCH - 1))
                if sample:
                    vdst = self.Vs[0:TP, sub, 4 * hv:4 * hv + 4, :]
                    vkey = ("Vs", hv)
                else:
                    vs = (4 * t + sub) % 8
                    vdst = self.V[:, vs, 4 * hv:4 * hv + 4, :]
                    vkey = ("V", vs, hv)
                vdst = vdst.rearrange("p i (e n) -> p i e n", e=3)[:, :, 0:3:2, :]
                src = self.ps[0:TP, b, :].rearrange("p (i e n) -> p i e n", i=4, e=2)
                if hv == 0:
                    self.cp("dve", vdst, src, reads=[("ps", b)], writes=[vkey])
                else:
                    self.act(vdst, src, AF.Copy, reads=[("ps", b)], writes=[vkey])
                if (last or sample) and "novout" not in self.cfg.get("SKIP", ()):
                    oi = self.rot("ost", 3)
                    if hv == 0:
                        self.cp("dve", self.ost[0:TP, oi, :], self.ps[0:TP, b, :], reads=[("ps", b)], writes=[("ost", oi)])
                    else:
                        self.act(self.ost[0:TP, oi, :], self.ps[0:TP, b, :], AF.Copy, reads=[("ps", b)], writes=[("ost", oi)])
                    if sample:
                        dst = self.vsn[a, sub * 32:(sub + 1) * 32, hv * 512:(hv + 1) * 512]
                    else:
                        dst = self.vp[a, s, sub * 128:(sub + 1) * 128, hv * 512:(hv + 1) * 512]
                    if "novdma" not in self.cfg.get("SKIP", ()):
                        self.dma("sp", dst, self.ost[0:TP, oi, :], reads=[("ost", oi)], writes=[])

    def attn_tile(self, L, t):
        a = L // 2
        W = 512
        steps = []
        for i in range(8):
            ms = [4 * t] + [m for m in (4 * t - 1, 4 * t + 1, 4 * t - 2, 4 * t + 2, 4 * t - 3, 4 * t + 3, 4 * t - 4) if m >= 0]
            for idx, m in enumerate(ms):
                steps.append((i, idx, m, idx == len(ms) - 1))
        accs = {}
        pend = None

        def qk(step):
            i, idx, m, lastm = step
            r0 = max(0, 8 * t - 2 * m)
            r1 = min(10, 8 * t - 2 * m + 8)
            w = (r1 - r0) * 64
            c0 = (2 * m + r0 - 8 * t) * 64
            kslot = (m // 4) % 2
            kc0 = kslot * 512 + (m % 4) * 128
            b0 = 2 * self.rot("stpair", 2)
            pi0 = 2 * self.rot("ptpair", 2)
            rn = min(r1, 4)
            for par in range(2):
                hd = 2 * i + par
                p0 = 64 * par
                b = b0 + par
                extra = []
                if r1 == 10:
                    extra.append("mask")
                if r0 < rn:
                    extra.append("bias")
                lastpair = (par == 1)
                self.mm(self.ps[:, b, 0:w], self.kT[:, i, kc0:kc0 + 128], self.qT[:, i, par, c0:c0 + w],
                        True, not extra, reads=[("kT", i, kslot), ("q", i)], writes=[("ps", b)],
                        inc=(lastpair and not extra))
                for ei, ex in enumerate(extra):
                    fin = (ei == len(extra) - 1)
                    if ex == "mask":
                        self.mm(self.ps[:, b, w - 64:w], self.identb[:], self.maskb[:], False, fin,
                                reads=["const"], writes=[("ps", b)], inc=(lastpair and fin))
                    else:
                        wn = (rn - r0) * 64
                        self.mm(self.ps[:, b, 0:wn], self.identb[:], self.biasb[:, hd, r0 * 64:rn * 64], False, fin,
                                reads=["const", ("biasn", a)], writes=[("ps", b)], inc=(lastpair and fin))
            self.act(self.PT[:, pi0:pi0 + 2, 0:w], self.ps[:, b0:b0 + 2, 0:w], AF.Exp,
                     reads=[("ps", b0), ("ps", b0 + 1)], writes=[("PT", pi0), ("PT", pi0 + 1)])
            return [(pi0, w, c0), (pi0 + 1, w, c0)]

        def pv(step, res):
            i, idx, m, lastm = step
            vs = m % 8
            if idx == 0:
                accs[i] = (self.bank("acc"), self.bank("acc"))
            for par in range(2):
                pi, w, c0 = res[par]
                b = accs[i][par]
                lo = 64 * par
                self.mm(self.ps[:, b, c0:c0 + w], self.V[:, vs, i, lo:lo + 128], self.PT[:, pi, 0:w], idx == 0, lastm,
                        reads=[("PT", pi), ("V", vs, i // 4)], writes=[("ps", b)], inc=True)
            if lastm and "nofin" not in self.cfg.get("SKIP", ()):
                be, bo = accs[i]
                ri = self.rot("rec", 1)
                rec = self.rec[:, ri, :]
                self.act(rec[0:64, :], self.ps[64:128, be, :], AF.Copy, reads=[("ps", be)], writes=[("rec", ri, 0)])
                self.act(rec[64:128, :], self.ps[0:64, bo, :], AF.Copy, reads=[("ps", bo)], writes=[("rec", ri, 1)])
                self.S.op("dve", lambda e, o=rec: e.reciprocal(o, o),
                          reads=[("rec", ri, 0), ("rec", ri, 1)], writes=[("rec", ri, 0), ("rec", ri, 1)])
                xk = [("xin", 0), ("xin", 1)]
                self.tt("dve", self.oT[0:64, i, 0:W], self.ps[0:64, be, :], rec[0:64, :], ALU.mult,
                        reads=[("ps", be), ("rec", ri, 0)], writes=[("oT", i)] + xk)
                self.tt("dve", self.oT[64:128, i, 0:W], self.ps[64:128, bo, :], rec[64:128, :], ALU.mult,
                        reads=[("ps", bo), ("rec", ri, 1)], writes=[("oT", i)] + xk)

        prev = None
        for st in steps:
            res = qk(st)
            if prev is not None:
                pv(*prev)
            prev = (st, res)
        pv(*prev)

    def out_proj(self, L, t, names, xsel, xkeys, W, bcol=None, src=None, srck=None):
        h = self.h if src is None else src
        hk = [("h", j) for j in range(NCH)] if srck is None else srck
        for c in range(NCH):
            if c % 4 == 0:
                wsl, wkey = self.wnext(L, names[c // 4])
                w3 = wsl.rearrange("p (kc n) -> p kc n", kc=8)
            cc = c % 4
            b = self.proj_feat(wsl, wkey, lambda kc, w3=w3, cc=cc: w3[:, kc, cc * 128:(cc + 1) * 128],
                               lambda kc: h[:, kc, 0:W], hk, W)
            if bcol is None:
                self.tt("dve", xsel(c), self.ps[:, b, 0:W], xsel(c), ALU.add, reads=[("ps", b), xkeys[c]], writes=[xkeys[c]])
            else:
                self.stt("dve", xsel(c), self.ps[:, b, 0:W], self.vecs[:, bcol + c:bcol + c + 1], xsel(c), ALU.add, ALU.add,
                         reads=[("ps", b), xkeys[c], "const"], writes=[xkeys[c]])

    def ffn_norm(self, L, tiles, W, h_alias=(), sq_alias=()):
        h2k = [("h2", j) for j in range(NCH)]
        for ti, (xap, xkeys, xsel) in enumerate(tiles):
            self.rmsnorm(xap, xkeys, V_NFFN + L * 8, W, self.h2, h2k, slice(ti * W, (ti + 1) * W),
                         self.sqF, [("sqF", j) for j in range(NCH)] + list(sq_alias), hextra=h_alias)

    def ffn(self, L, tiles, W):
        self.ffn_norm(L, tiles, W)
        self.ffn_p1(L, tiles, W)
        self.ffn_p2(L, tiles, W)

    def ffn_p1(self, L, tiles, W):
        nt = len(tiles)
        h2 = self.h2
        h2k = [("h2", j) for j in range(NCH)]
        for ii in range(11):
            wsl, wkey = self.wnext(L, "i%d" % ii)
            w4 = wsl.rearrange("p (kc q n) -> p kc q n", kc=8, q=4)
            for jj in range(2):
                j = 2 * ii + jj
                for ti in range(nt):
                    bg = self.bank("all")
                    for kc in range(NCH):
                        self.mm(self.ps[:, bg, 0:W], w4[:, kc, 2 * jj, :], h2[:, kc, ti * W:(ti + 1) * W], kc == 0, kc == NCH - 1,
                                reads=[wkey] + h2k, writes=[("ps", bg)], inc=(kc == NCH - 1))
                    bu = self.bank("all")
                    for kc in range(NCH):
                        self.mm(self.ps[:, bu, 0:W], w4[:, kc, 2 * jj + 1, :], h2[:, kc, ti * W:(ti + 1) * W], kc == 0, kc == NCH - 1,
                                reads=[wkey] + h2k, writes=[("ps", bu)], inc=(kc == NCH - 1))
                    si = self.rot("sg", 3)
                    self.act(self.sg[:, si, 0:W], self.ps[:, bg, 0:W], AF.Silu, reads=[("ps", bg)], writes=[("sg", si)])
                    self.tt("dve", self.aT[:, j, ti * W:(ti + 1) * W], self.sg[:, si, 0:W], self.ps[:, bu, 0:W], ALU.mult,
                            reads=[("sg", si), ("ps", bu)], writes=[("aT", j)])
    def ffn_p2(self, L, tiles, W):
        aTk = [("aT", j) for j in range(NFF)]
        for o in range(NCH):
            wsl, wkey = self.wnext(L, "f%d" % o)
            w3 = wsl[:, 0:NFF * 128].rearrange("p (kc n) -> p kc n", kc=NFF)
            for ti, (xap, xkeys, xsel) in enumerate(tiles):
                b = self.bank("all")
                for kc in range(NFF):
                    self.mm(self.ps[:, b, 0:W], w3[:, kc, :], self.aT[:, kc, ti * W:(ti + 1) * W], kc == 0, kc == NFF - 1,
                            reads=[wkey] + aTk, writes=[("ps", b)], inc=(kc == NFF - 1))
                self.tt("dve", xsel(o), self.ps[:, b, 0:W], xsel(o), ALU.add, reads=[("ps", b), xkeys[o]], writes=[xkeys[o]])

    def conv_tile(self, L, s, t, xap, xkeys, xsel, sample=False):
        c = L // 2
        W = 128 if sample else 512
        h = self.h
        hk = [("h", j) for j in range(NCH)]
        last = (not sample) and (t == self.cfg["SEQ"] // 512 - 1)
        u = self.u
        UW = 248 if sample else 30 + 512
        sqc = self.y[:, 0:4, :].bitcast(BF16).rearrange("p a (b n) -> p (a b) n", b=2)
        self.rmsnorm(xap, xkeys, V_NMIX + L * 8, W, h, hk, slice(0, W), sqc, [("y", j) for j in range(4)])
        yield "R"
        if sample:
            pass
        elif t == 0:
            self.memset("dve", u[:, :, 0:30], 0.0, writes=[("u", j) for j in range(NCH)])
        else:
            self.cp("dve", u[:, :, 0:30], u[:, :, 512:542], reads=[("u", j) for j in range(NCH)],
                    writes=[("u", j) for j in range(NCH)])
        for pi in range(4):
            wsl, wkey = self.wnext(L, "p%d" % pi)
            w4 = wsl.rearrange("p (kc q n) -> p kc q n", kc=8, q=4)
            for jj in range(2):
                j = 2 * pi + jj
                ba = self.proj_feat(wsl, wkey, lambda kc, w4=w4, jj=jj: w4[:, kc, 2 * jj, :], lambda kc: h[:, kc, 0:W], hk, W)
                bg = self.proj_feat(wsl, wkey, lambda kc, w4=w4, jj=jj: w4[:, kc, 2 * jj + 1, :], lambda kc: h[:, kc, 0:W], hk, W)
                si = self.rot("sig", 2)
                bcol_a = V_PW1B + c * 16 + j
                bcol_g = V_PW1B + c * 16 + 8 + j
                self.act(self.sig[:, si, 0:W], self.ps[:, bg, 0:W], AF.Sigmoid, reads=[("ps", bg), "const"],
                         writes=[("sig", si)], bias=self.vecs[:, bcol_g:bcol_g + 1])
                if sample:
                    udst = u[:, j, 0:248].rearrange("p (b n) -> p b n", b=4)[:, :, 30:62]
                    psa = self.ps[:, ba, 0:W].rearrange("p (b n) -> p b n", b=4)
                    sg_ = self.sig[:, si, 0:W].rearrange("p (b n) -> p b n", b=4)
                else:
                    udst = u[:, j, 30:30 + W]
                    psa = self.ps[:, ba, 0:W]
                    sg_ = self.sig[:, si, 0:W]
                self.stt("dve", udst, psa, self.vecs[:, bcol_a:bcol_a + 1], sg_, ALU.add, ALU.mult,
                         reads=[("ps", ba), ("sig", si), "const"], writes=[("u", j)])
                if last or sample:
                    self.stt("dve", self.uf[:, j, 0:W if sample else 32],
                             self.ps[:, ba, 0:W] if sample else self.ps[:, ba, W - 32:W],
                             self.vecs[:, bcol_a:bcol_a + 1],
                             self.sig[:, si, 0:W] if sample else self.sig[:, si, W - 32:W], ALU.add, ALU.mult,
                             reads=[("ps", ba), ("sig", si), "const"], writes=[("uf", j)])
        if last or sample:
            nb = 4 if sample else 1
            for bb in range(nb):
                for g4 in range(2):
                    b = self.bank("mm")
                    for q in range(4):
                        j = g4 * 4 + q
                        self.tr(self.ps[0:32, b, q * 128:(q + 1) * 128], self.uf[:, j, bb * 32:(bb + 1) * 32], self.identf[:],
                                reads=[("uf", j), "const"], writes=[("ps", b)], inc=(q == 3))
                    oi = self.rot("ost", 3)
                    self.act(self.ost[0:32, oi, :], self.ps[0:32, b, :], AF.Copy, reads=[("ps", b)], writes=[("ost", oi)])
                    if sample:
                        dst = self.csn[c, bb, :, g4 * 512:(g4 + 1) * 512]
                    else:
                        dst = self.cpn[c, s, :, g4 * 512:(g4 + 1) * 512]
                    self.dma("sp", dst, self.ost[2:32, oi, :], reads=[("ost", oi)], writes=[])
        yield "P"
        NO = (UW - 30) if sample else W
        for j in range(NCH):
            di = self.rot("dg", 2)
            col = V_DWW + (c * 8 + j) * CW
            ib = self.identb[:]
            vb = self.vecs[:, col:col + CW]
            in0 = bass.AP(ib.tensor, ib.offset, [list(ib.ap[0]), [0, CW], list(ib.ap[1])])
            in1 = bass.AP(vb.tensor, vb.offset, [list(vb.ap[0]), list(vb.ap[1]), [0, 128]])
            self.tt("dve", self.dg[:, di, :, :], in0, in1, ALU.mult, reads=["const"], writes=[("dg", di)])
            b = self.bank("acc")
            for w in range(CW):
                self.mm(self.ps[:, b, 0:NO], self.dg[:, di, w, :], u[:, j, w:w + NO], w == 0, w == CW - 1,
                        reads=[("dg", di), ("u", j)], writes=[("ps", b)], inc=(w == CW - 1))
            bcol = V_DWB + c * 8 + j
            if sample:
                src = self.ps[:, b, 0:248].rearrange("p (b n) -> p b n", b=4)[:, :, 0:32]
                ydst = self.y[:, j, 0:W].rearrange("p (b n) -> p b n", b=4)
                ybd = self.ybf[:, j, 0:W].rearrange("p (b n) -> p b n", b=4)
                ysd = self.ysq[:, j, 0:W].rearrange("p (b n) -> p b n", b=4)
            else:
                src = self.ps[:, b, 0:W]
                ydst = self.y[:, j, 0:W]
                ybd = self.ybf[:, j, 0:W]
                ysd = self.ysq[:, j, 0:W]
            self.act(ydst, src, AF.Identity, reads=[("ps", b), "const"], writes=[("y", j)], bias=self.vecs[:, bcol:bcol + 1])
            self.act(ybd, ydst, AF.Copy, reads=[("y", j)], writes=[("ybf", j)])
            self.act(ysd, ydst, AF.Square, reads=[("y", j)], writes=[("ysq", j)])
        b1 = self.bank("mm")
        for j in range(NCH):
            self.mm(self.ps[:, b1, 0:W], self.meanmat[:], self.ybf[:, j, 0:W], j == 0, j == NCH - 1,
                    reads=[("ybf", j), "const"], writes=[("ps", b1)], inc=(j == NCH - 1))
        b2 = self.bank("mm")
        for j in range(NCH):
            self.mm(self.ps[:, b2, 0:W], self.meanmat[:], self.ysq[:, j, 0:W], j == 0, j == NCH - 1,
                    reads=[("ysq", j), "const"], writes=[("ps", b2)], inc=(j == NCH - 1))
        mean = self.st[:, 0, 0:W]
        msq = self.st[:, 1, 0:W]
        rstd = self.st[:, 2, 0:W]
        self.cp("dve", mean, self.ps[:, b1, 0:W], reads=[("ps", b1)], writes=[("st", 0)])
        self.tt("dve", msq, mean, mean, ALU.mult, reads=[("st", 0)], writes=[("st", 1)])
        self.tt("dve", rstd, self.ps[:, b2, 0:W], msq, ALU.subtract, reads=[("ps", b2), ("st", 1)], writes=[("st", 2)])
        self.rsqrt(rstd, rstd, reads=[("st", 2)], wkey=("st", 2))
        yield "B"
        ybk = [("ybf", j) for j in range(NCH)]
        for j in range(NCH):
            yj = self.y[:, j, 0:W]
            eng = "dve"
            self.tt(eng, yj, yj, mean, ALU.subtract, reads=[("y", j), ("st", 0)], writes=[("y", j)])
            self.tt(eng, yj, yj, rstd, ALU.mult, reads=[("y", j), ("st", 2)], writes=[("y", j)])
            gcol = V_LNG + c * 8 + j
            bcol = V_LNB + c * 8 + j
            self.act(self.ybf[:, j, 0:W], yj, AF.Silu, reads=[("y", j), "const"], writes=[("ybf", j)],
                     bias=self.vecs[:, bcol:bcol + 1], scale=self.vecs[:, gcol:gcol + 1])
        yield "C1"
        self.out_proj(L, t, ["w0", "w1"], xsel, xkeys, W, bcol=V_PW2B + c * 8, src=self.ybf, srck=ybk)
        yield "C2"

    def sample_cache_load(self, k):
        if k >= len(self.sjobs):
            return
        L, bb = self.sjobs[k]
        a = L // 2
        buf = k % 2
        self.dma("sp", self.ck32[buf], self.ck[a, bb].rearrange("(m p) n -> p m n", p=128), reads=[], writes=[("ck32", buf)])
        self.dma("sp", self.cv32[buf], self.cv[a, bb].rearrange("(m p) n -> p m n", p=128), reads=[], writes=[("cv32", buf)])

    def attn_sample(self, L):
        a = L // 2
        for bb in range(4):
            k = self.sjobs.index((L, bb))
            buf = k % 2
            if k == 0:
                self.sample_cache_load(0)
            self.sample_cache_load(k + 1)
            for m in range(4):
                for g4 in range(2):
                    b = self.bank("mm")
                    for q in range(4):
                        j = g4 * 4 + q
                        self.tr(self.ps[:, b, q * 128:(q + 1) * 128], self.ck32[buf][:, m, j * 128:(j + 1) * 128], self.identf[:],
                                reads=[("ck32", buf), "const"], writes=[("ps", b)], inc=(q == 3))
                    dst = self.kc[:, g4 * 4:g4 * 4 + 4, m * 128:(m + 1) * 128]
                    src = self.ps[:, b, :].rearrange("p (q n) -> p q n", q=4)
                    if g4 == 0:
                        self.cp("dve", dst, src, reads=[("ps", b)], writes=[("kc", m)])
                    else:
                        self.act(dst, src, AF.Copy, reads=[("ps", b)], writes=[("kc", m)])
                vsrc = self.cv32[buf][:, m, :].rearrange("p (i e n) -> p i e n", i=8, e=2)
                vdst = self.Vc[:, m, :, :].rearrange("p i (e n) -> p i e n", e=3)[:, :, 0:3:2, :]
                if m % 2 == 0:
                    self.cp("dve", vdst, vsrc, reads=[("cv32", buf)], writes=["Vc"])
                else:
                    self.act(vdst, vsrc, AF.Copy, reads=[("cv32", buf)], writes=["Vc"])
            q0 = bb * 32
            for i in range(8):
                res = []
                for par in range(2):
                    hd = 2 * i + par
                    p0 = 64 * par
                    b = self.bank("mm")
                    kcr = [("kc", mm_) for mm_ in range(4)] + [("q", i)]
                    self.mm(self.ps[:, b, 96:128], self.kc[p0:p0 + 64, i, 384:512], self.qT[p0:p0 + 64, i, par, q0:q0 + 32],
                            True, False, reads=kcr, writes=[("ps", b)], inc=False)
                    self.mm(self.ps[:, b, 96:128], self.identb[:], self.biasb[:, hd, 128:160], False, True,
                            reads=["const", ("biasn", a)], writes=[("ps", b)], inc=False)
                    self.mm(self.ps[0:32, b, 128:160], self.kTs[p0:p0 + 64, i, q0:q0 + 32], self.qT[p0:p0 + 64, i, par, q0:q0 + 32],
                            True, False, reads=[("kTs", i), ("q", i)], writes=[("ps", b)], inc=False)
                    self.mm(self.ps[0:32, b, 128:160], self.identb[0:32, 0:32], self.biasb[0:32, hd, 0:32], False, True,
                            reads=["const", ("biasn", a)], writes=[("ps", b)], inc=False)
                    for m in range(3):
                        self.mm(self.ps[:, b, m * 32:(m + 1) * 32], self.kc[p0:p0 + 64, i, m * 128:(m + 1) * 128],
                                self.qT[p0:p0 + 64, i, par, q0:q0 + 32], True, True, reads=kcr, writes=[("ps", b)], inc=(m == 2))
                    pi = self.rot("PT", 4)
                    self.act(self.PT[:, pi, 0:128], self.ps[:, b, 0:128], AF.Exp, reads=[("ps", b)], writes=[("PT", pi)])
                    self.act(self.PT[0:32, pi, 128:160], self.ps[0:32, b, 128:160], AF.Exp, reads=[("ps", b)],
                             writes=[("PT", pi)])
                    res.append(pi)
                accb = (self.bank("acc"), self.bank("acc"))
                for par in range(2):
                    pi = res[par]
                    b = accb[par]
                    lo = 64 * par
                    for m in range(4):
                        self.mm(self.ps[:, b, 0:32], self.Vc[:, m, i, lo:lo + 128], self.PT[:, pi, m * 32:(m + 1) * 32],
                                m == 0, False, reads=[("PT", pi), "Vc"], writes=[("ps", b)], inc=False)
                    self.mm(self.ps[:, b, 0:32], self.Vs[0:32, bb, i, lo:lo + 128], self.PT[0:32, pi, 128:160],
                            False, True, reads=[("PT", pi), ("Vs", i // 4)], writes=[("ps", b)], inc=True)
                be, bo = accb
                ri = self.rot("rec", 1)
                rec = self.rec[:, ri, 0:32]
                self.S.op("dve", lambda e, o=rec[0:64, :], x=self.ps[64:128, be, 0:32]: e.reciprocal(o, x),
                          reads=[("ps", be)], writes=[("rec", ri, 0)])
                self.S.op("dve", lambda e, o=rec[64:128, :], x=self.ps[0:64, bo, 0:32]: e.reciprocal(o, x),
                          reads=[("ps", bo)], writes=[("rec", ri, 1)])
                self.tt("dve", self.oT[0:64, i, q0:q0 + 32], self.ps[0:64, be, 0:32], rec[0:64, :], ALU.mult,
                        reads=[("ps", be), ("rec", ri, 0)], writes=[("oT", i)])
                self.tt("dve", self.oT[64:128, i, q0:q0 + 32], self.ps[64:128, bo, 0:32], rec[64:128, :], ALU.mult,
                        reads=[("ps", bo), ("rec", ri, 1)], writes=[("oT", i)])

    def load_bias(self, a, stkeys):
        bk = [("biasn", 0), ("biasn", 1)]
        self.dma("sp", self.biasst[:], self.biasn_d[a], reads=[], writes=stkeys)
        cb = self.chb[:, a, :]
        cbb = bass.AP(cb.tensor, cb.offset, [list(cb.ap[0]), list(cb.ap[1]), [0, 256]])
        self.tt("dve", self.biasb[:], self.biasst[:], cbb, ALU.subtract, reads=stkeys + ["const"], writes=bk)
        self.memset("dve", self.biasb[64:128, :, 0:64], NEG, writes=bk)
        qk_ = [("q", c) for c in range(NCH)]
        self.memset("dve", self.qT[64:128, :, 0, :], 0.0, writes=qk_)
        self.memset("dve", self.qT[0:64, :, 1, :], 0.0, writes=qk_)

    def weight_sequence(self):
        cfg = self.cfg
        seq = []
        NT = cfg["SEQ"] // 512

        def mixer_tiles(L):
            if L % 2 == 0:
                return [(L, n) for n in ["q0", "q1", "k0", "k1", "v0", "v1", "o0", "o1"]]
            return [(L, n) for n in ["p0", "p1", "p2", "p3", "w0", "w1"]]

        def ffn_tiles(L):
            return [(L, "i%d" % i) for i in range(11)] + [(L, "f%d" % i) for i in range(8)]

        PH = cfg.get("PH", ("attn", "conv", "ffn"))
        for s in range(cfg["NBP"]):
            for L in range(cfg["DEPTH"]):
                if L % 2 == 0 and "attn" in PH:
                    qkv = [(L, n) for n in ["q0", "q1", "k0", "k1", "v0", "v1"]]
                    wo = [(L, "o0"), (L, "o1")]
                    seq += qkv
                    for t in range(NT):
                        if t + 1 < NT:
                            seq += qkv
                        seq += wo
                if L % 2 == 1 and "conv" in PH:
                    pw1 = [(L, "p%d" % i) for i in range(4)]
                    pw2 = [(L, "w0"), (L, "w1")]
                    seq += pw1
                    for t in range(1, NT):
                        seq += pw1 + pw2
                    seq += pw2
                if "ffn" in PH:
                    for blk in range(NT // 2):
                        seq += ffn_tiles(L)
        if cfg["SAMPLE"]:
            for L in range(cfg["DEPTH"]):
                if ("attn" if L % 2 == 0 else "conv") in PH:
                    seq += mixer_tiles(L)
                if "ffn" in PH:
                    seq += ffn_tiles(L)
        return seq

    def emit(self):
        cfg = self.cfg
        nc = self.nc
        NT = cfg["SEQ"] // 512
        self.plan_weights()
        self.wseq = self.weight_sequence()
        self.wpos = 0
        self.wissued = 0
        cw = ["const"]
        self.dma("sp", self.vecs[:], self.vecs_d[:, :], reads=[], writes=cw)
        self.dma("sp", self.chb[:], self.chb_d[:, :, :], reads=[], writes=cw)
        self.dma("sp", self.identf[:], self.ident_d[:, :], reads=[], writes=cw)
        self.emit_casts()
        self.cp("dve", self.identb[:], self.identf[:], reads=cw, writes=cw)
        self.memset("dve", self.meanmat[:], 1.0 / 1024.0, writes=cw)
        self.memset("dve", self.blkmean[:], 0.0, writes=cw)
        self.memset("dve", self.blkmean[0:64, 0:64], 1.0 / 64.0, writes=cw)
        self.memset("dve", self.blkmean[64:128, 64:128], 1.0 / 64.0, writes=cw)
        self.memset("dve", self.epsc[:], EPS, writes=cw)
        self.memset("dve", self.maskb[:], 0.0, writes=cw)
        self.memset("dve", self.maskb[0:64, :], NEG, writes=cw)
        self.ts("dve", self.gq8[:], self.vecs[:, V_GQ:V_GQ + 2], 0.125, None, ALU.mult, None, reads=cw, writes=cw)

        def load_bias(a):
            self.load_bias(a, [("V", vs, hv) for vs in range(8) for hv in range(2)])
            self.memset("dve", self.V[:, :, :, 64:128], 1.0, writes=[("V", vs, hv) for vs in range(8) for hv in range(2)])

        PH = cfg.get("PH", ("attn", "conv", "ffn"))
        self.PH = PH
        for s in range(cfg["NBP"]):
            if s == 0 or "ffn" not in PH:
                self.load_x_seq(s)
            for L in range(cfg["DEPTH"]):
                blks = []
                for blk in range(NT // 2):
                    tiles = []
                    for t in (2 * blk, 2 * blk + 1):
                        cols = slice(t * 512, (t + 1) * 512)
                        tiles.append((self.xT[:, :, cols], [("x", j, t) for j in range(NCH)],
                                      lambda c, cols=cols: self.xT[:, c, cols]))
                    blks.append(tiles)
                hoisted = False
                if L % 2 == 0 and "attn" in PH:
                    load_bias(L // 2)
                if L % 2 == 1 and "conv" in PH:
                    gens = []
                    for t in range(NT):
                        cols = slice(t * 512, (t + 1) * 512)
                        gens.append(self.conv_tile(L, s, t, self.xT[:, :, cols], [("x", j, t) for j in range(NCH)],
                                                   lambda c, cols=cols: self.xT[:, c, cols]))
                    next(gens[0]); next(gens[0])
                    for t in range(NT):
                        if t + 1 < NT:
                            next(gens[t + 1])
                        next(gens[t]); next(gens[t])
                        if t + 1 < NT:
                            next(gens[t + 1])
                        elif "ffn" in PH:
                            self.ffn_norm(L, blks[0], 512,
                                          h_alias=[("u", j) for j in range(NCH)] + [("dg", 0), ("dg", 1)],
                                          sq_alias=[("sig", 0), ("sig", 1)] + [("uf", j) for j in range(NCH)])
                            hoisted = True
                        next(gens[t])
                for t in range(NT):
                    if ("attn" if L % 2 == 0 else "conv") not in PH or L % 2 == 1:
                        continue
                    cols = slice(t * 512, (t + 1) * 512)
                    xap = self.xT[:, :, cols]
                    xkeys = [("x", j, t) for j in range(NCH)]
                    xsel = lambda c, cols=cols: self.xT[:, c, cols]
                    pass
                if L % 2 == 0 and "attn" in PH:
                    oTk = [("oT", j) for j in range(NCH)] + [("xin", 0), ("xin", 1)]

                    def a_norm(t):
                        cols = slice(t * 512, (t + 1) * 512)
                        self.rmsnorm(self.xT[:, :, cols], [("x", j, t) for j in range(NCH)], V_NMIX + L * 8, 512, self.h,
                                     [("h", j) for j in range(NCH)], slice(0, 512), self.sqA, self.sqAk)

                    def a_out(t):
                        cols = slice(t * 512, (t + 1) * 512)
                        self.out_proj(L, t, ["o0", "o1"], lambda c, cols=cols: self.xT[:, c, cols],
                                      [("x", j, t) for j in range(NCH)], 512, src=self.oT, srck=oTk)

                    a_norm(0)
                    self.qkv_tile(L, s, 0)
                    for t in range(NT):
                        if t + 1 < NT:
                            a_norm(t + 1)
                        self.attn_tile(L, t)
                        if t + 1 < NT:
                            self.qkv_tile(L, s, t + 1)
                        elif "ffn" in PH:
                            self.ffn_norm(L, blks[0], 512,
                                          h_alias=[("q", c) for c in range(NCH)],
                                          sq_alias=[("biasn", 0), ("biasn", 1)] + [("PT", i) for i in range(4)]
                                          + [("rec", 0, 0), ("rec", 0, 1), ("sqh", 0), ("sqh", 1)])
                            hoisted = True
                        a_out(t)
                self.S.barrier()
                if "ffn" in PH:
                    if not hoisted:
                        self.ffn_norm(L, blks[0], 512)
                    for bi, tiles in enumerate(blks):
                        self.ffn_p1(L, tiles, 512)
                        if bi + 1 < len(blks):
                            self.ffn_norm(L, blks[bi + 1], 512)
                        self.ffn_p2(L, tiles, 512)
                        if L == cfg["DEPTH"] - 1:
                            tts = range(8 * bi, 8 * bi + 8)
                            self.store_y_seq(s, tts)
                            if s + 1 < cfg["NBP"]:
                                self.load_x_seq(s + 1, tts)
                self.S.barrier()
            if "ffn" not in PH:
                self.store_y_seq(s)
        if cfg["SAMPLE"]:
            self.S.barrier()
            self.sample_block()
        assert self.wpos == len(self.wseq)
        fin = []
        for i in range(NDMASEM):
            if self.S.dcnt[i] > 0:
                fin.append((("d", i), self.S.dcnt[i]))
        self.S.ops["sp"].append((fin, None, None, 0))

    def sample_block(self):
        cfg = self.cfg
        self.sjobs = [(L, bb) for L in range(0, cfg["DEPTH"], 2) for bb in range(4)]
        xs = self.xsT
        xkeys = [("xs", j) for j in range(NCH)]
        xsel = lambda c: xs[:, c, :]
        self.dma("sp", self.xin[:, 0, :], self.xs_d[:, :], reads=[], writes=[("xin", 0)])
        for g4 in range(2):
            b = self.bank("mm")
            for q in range(4):
                j = g4 * 4 + q
                self.tr(self.ps[:, b, q * 128:(q + 1) * 128], self.xin[:, 0, j * 128:(j + 1) * 128], self.identf[:],
                        reads=[("xin", 0), "const"], writes=[("ps", b)], inc=(q == 3))
            self.cp("dve", xs[:, g4 * 4:(g4 + 1) * 4, :], self.ps[:, b, :].rearrange("p (q n) -> p q n", q=4),
                    reads=[("ps", b)], writes=[("xs", j) for j in range(g4 * 4, g4 * 4 + 4)])
        for L in range(cfg["DEPTH"]):
            if ("attn" if L % 2 == 0 else "conv") not in self.PH:
                pass
            elif L % 2 == 0:
                a = L // 2
                self.load_bias(a, ["Vc", ("Vs", 0), ("Vs", 1)])
                self.memset("dve", self.Vs[:, :, :, 64:128], 1.0, writes=[("Vs", 0), ("Vs", 1)])
                self.memset("dve", self.Vc[:, :, :, 64:128], 1.0, writes=["Vc"])
                self.rmsnorm(xs[:, :, :], xkeys, V_NMIX + L * 8, 128, self.h, [("h", j) for j in range(NCH)], slice(0, 128),
                             self.sqA, self.sqAk)
                self.qkv_tile(L, 0, 0, sample=True)
                self.attn_sample(L)
                self.out_proj(L, 0, ["o0", "o1"], xsel, xkeys, 128, src=self.oT, srck=[("oT", j) for j in range(NCH)])
            else:
                c = L // 2
                for bb in range(4):
                    xi = self.rot("xin", 2)
                    self.dma("sp", self.xin[0:30, xi, :], self.sc[c, bb], reads=[], writes=[("xin", xi)])
                    for g4 in range(2):
                        b = self.bank("mm")
                        for q in range(4):
                            j = g4 * 4 + q
                            self.tr(self.ps[:, b, q * 128:q * 128 + 30], self.xin[0:30, xi, j * 128:(j + 1) * 128],
                                    self.identf[0:30, 0:30], reads=[("xin", xi), "const"], writes=[("ps", b)], inc=(q == 3))
                        dst = self.u[:, g4 * 4:g4 * 4 + 4, bb * 62:bb * 62 + 30]
                        src = self.ps[:, b, :].rearrange("p (q n) -> p q n", q=4)[:, :, 0:30]
                        self.cp("dve", dst, src, reads=[("ps", b)], writes=[("u", j) for j in range(g4 * 4, g4 * 4 + 4)])
                for _ in self.conv_tile(L, 0, 0, xs[:, :, :], xkeys, xsel, sample=True):
                    pass
            self.S.barrier()
            if "ffn" in self.PH:
                self.ffn(L, [(xs[:, :, :], xkeys, xsel)], 128)
            self.S.barrier()
        for g4 in range(2):
            b = self.bank("mm")
            for q in range(4):
                j = g4 * 4 + q
                self.tr(self.ps[:, b, q * 128:(q + 1) * 128], xs[:, j, :], self.identf[:],
                        reads=[("xs", j), "const"], writes=[("ps", b)], inc=(q == 3))
            oi = self.rot("ost", 3)
            self.cp("dve", self.ost[:, oi, :], self.ps[:, b, :], reads=[("ps", b)], writes=[("ost", oi)])
            self.dma("sp", self.ys_d[:, g4 * 512:(g4 + 1) * 512], self.ost[:, oi, :], reads=[("ost", oi)], writes=[])


def build(cfg):
    nc = bass.Bass("TRN2", target_bir_lowering=False)
    g = Gen(nc, cfg)
    NBP, SEQ, DEPTH = cfg["NBP"], cfg["SEQ"], cfg["DEPTH"]
    NA = (DEPTH + 1) // 2
    NC_ = DEPTH // 2

    def din(name, shape, dt=F32):
        return nc.dram_tensor(name, list(shape), dt, kind="ExternalInput").ap()

    def dout(name, shape, dt=F32):
        return nc.dram_tensor(name, list(shape), dt, kind="ExternalOutput").ap()

    g.xp = din("xp", [NBP, SEQ, D])
    g.xs_d = din("xs", [128, D])
    g.ck = din("ck", [NA, 4, 512, D])
    g.cv = din("cv", [NA, 4, 512, D])
    g.sc = din("sc", [max(NC_, 1), 4, 30, D])
    g.W = {
        "w_qkv": din("w_qkv", [NA, D, 3 * D]),
        "w_o": din("w_o", [NA, D, D]),
        "pw1_w": din("pw1_w", [max(NC_, 1), D, 2 * D]),
        "pw2_w": din("pw2_w", [max(NC_, 1), D, D]),
        "ffn_w_in": din("ffn_w_in", [DEPTH, D, 2 * DFF]),
        "ffn_w_out": din("ffn_w_out", [DEPTH, DFF, D]),
    }
    g.vecs_d = din("vecs", [128, NV])
    g.chb_d = din("chb", [128, 2, NH])
    g.biasn_d = din("biasn", [2, 128, NH, 256])
    g.ident_d = din("ident", [128, 128])
    g.yp = dout("yp", [NBP, SEQ, D])
    g.ys_d = dout("ys", [128, D])
    g.kp = dout("kp", [NA, NBP, 512, D])
    g.vp = dout("vp", [NA, NBP, 512, D])
    g.ksn = dout("ksn", [NA, 128, D])
    g.vsn = dout("vsn", [NA, 128, D])
    g.cpn = dout("cpn", [max(NC_, 1), NBP, 30, D])
    g.csn = dout("csn", [max(NC_, 1), 4, 30, D])
    g.plan_weights()
    g.wscr = nc.dram_tensor("wscr", [g.ntiles, 128, 4096], BF16).ap()

    with contextlib.ExitStack() as st:
        def sb(name, shape, dt):
            return st.enter_context(nc.sbuf_tensor(name, list(shape), dt))

        g.xT = sb("xT", [128, NCH, SEQ], F32)
        g.h = sb("h", [128, NCH, 512], BF16)
        g.wslots = sb("wslots", [128, NSLOT, 4096], BF16)
        g.ost = sb("ost", [128, 3, 512], F32)
        g.vecs = sb("vecs_sb", [128, NV], F32)
        g.chb = sb("chb_sb", [128, 2, NH], F32)
        g.identf = sb("identf", [128, 128], F32)
        g.identb = sb("identb", [128, 128], BF16)
        g.meanmat = sb("meanmat", [128, 128], BF16)
        g.blkmean = sb("blkmean", [128, 128], BF16)
        g.gq8 = sb("gq8", [128, 2], F32)
        g.epsc = sb("epsc", [128, 1], F32)
        g.maskb = sb("maskb", [128, 64], BF16)
        g.rstd = sb("rstd", [128, 512], F32)
        ASZ = 86 * 1024
        arena = sb("arena", [128, ASZ], mybir.dt.uint8)
        off = [0]

        def carve(shape, dt, reset=None):
            if reset is not None:
                off[0] = reset
            n = int(np.prod(shape[1:])) * (4 if dt == F32 else 2)
            n = (n + 63) // 64 * 64
            ap = arena[:, off[0]:off[0] + n].bitcast(dt)
            off[0] += n
            assert off[0] <= ASZ, (off[0], ASZ)
            if len(shape) == 2:
                return ap
            if len(shape) == 3:
                return ap.rearrange("p (a b) -> p a b", a=shape[1])
            if len(shape) == 4:
                return ap.rearrange("p (a b c) -> p a b c", a=shape[1], b=shape[2])
            raise ValueError

        g.qT = carve([128, NCH, 2, 512], BF16, reset=0)
        kT_off = off[0]
        g.kT = carve([128, NCH, 1024], BF16)
        V_off = off[0]
        g.V = carve([128, 8, 8, 192], BF16)
        g.biasst = carve([128, NH, 256], F32, reset=V_off)
        off[0] = V_off + 24 * 1024
        g.biasb = carve([128, NH, 256], BF16)
        sqA_off = off[0]
        g.PT = carve([128, 4, 512], BF16)
        g.rec = carve([128, 1, 512], F32)
        g.sqh = carve([128, 2, 512], BF16)
        g.sqA = carve([128, NCH, 512], BF16, reset=sqA_off)
        g.sqAk = [("PT", i) for i in range(4)] + [("rec", 0, 0), ("rec", 0, 1), ("sqh", 0), ("sqh", 1)]
        g.rs = carve([128, 2, 512], F32)
        g.kf = carve([128, 1, 512], F32)
        g.oT = carve([128, NCH, 512], BF16)
        xin_off = ASZ - 8 * 1024
        g.u = carve([128, NCH, 544], BF16, reset=0)
        g.dg = carve([128, 2, CW, 128], BF16)
        g.y = carve([128, NCH, 512], F32)
        g.ybf = carve([128, NCH, 512], BF16)
        g.ysq = carve([128, NCH, 512], BF16)
        g.st = carve([128, 3, 512], F32)
        g.sig = carve([128, 2, 512], F32)
        g.uf = carve([128, NCH, 128], F32)
        assert off[0] <= xin_off, (off[0], xin_off)
        g.h2 = carve([128, NCH, 1024], BF16, reset=0)
        g.aT = carve([128, NFF, 1024], BF16)
        g.sg = carve([128, 3, 512], BF16)
        g.sqF = carve([128, NCH, 512], BF16)
        assert off[0] <= xin_off, (off[0], xin_off)
        g.xin = carve([128, 2, 1024], F32, reset=xin_off)
        if cfg["SAMPLE"]:
            g.xsT = sb("xsT", [128, NCH, 128], F32)
            g.kc = carve([128, NCH, 512], BF16, reset=kT_off)
            xv = g.xT[:].rearrange("p a b -> p (a b)")
            g.ck32 = [xv[:, (2 * i) * 4096:(2 * i + 1) * 4096].rearrange("p (m n) -> p m n", m=4) for i in range(2)]
            g.cv32 = [xv[:, (2 * i + 1) * 4096:(2 * i + 2) * 4096].rearrange("p (m n) -> p m n", m=4) for i in range(2)]
            g.Vc = carve([128, 4, 8, 192], BF16, reset=V_off)
            g.Vs = carve([128, 4, 8, 192], BF16)
            assert off[0] <= V_off + 24 * 1024
            g.kTs = g.qT[:, :, 0, 128:256]
        g.ps_t = st.enter_context(nc.psum_tensor("ps", [128, 8, 512], F32))
        g.ps = g.ps_t

        g.emit()
        sems = {}
        for e in ENGS:
            sems[e] = st.enter_context(nc.semaphore("sem_" + e))
        for i in range(NDMASEM):
            sems[("d", i)] = st.enter_context(nc.semaphore("sem_d%d" % i))

        def replay(eng, e):
            for waits, fn, incsem, incv in g.S.ops[eng]:
                for s, v in waits:
                    e.wait_ge(sems[s], v)
                if fn is None:
                    continue
                ins = fn(e)
                if incsem is not None:
                    ins.then_inc(sems[incsem], incv)

        with nc.Block() as block:
            @block.sync
            def _(e):
                replay("sp", e)

            @block.tensor
            def _(e):
                replay("pe", e)

            @block.scalar
            def _(e):
                replay("act", e)

            @block.vector
            def _(e):
                replay("dve", e)

            @block.gpsimd
            def _(e):
                replay("pool", e)
    return nc, g


def _host_params(inp):
    vecs = np.zeros((128, NV), np.float32)

    def put(col, arr):
        a = np.asarray(arr, np.float32)
        n = a.shape[-1] // 128
        a2 = a.reshape(-1, n, 128)
        a2 = np.transpose(a2, (2, 0, 1)).reshape(128, -1)
        vecs[:, col:col + a2.shape[1]] = a2

    put(V_NMIX, inp["norm_mix"])
    put(V_NFFN, inp["norm_ffn"])
    put(V_PW1B, inp["pw1_b"])
    put(V_DWB, inp["dw_b"])
    put(V_LNG, inp["conv_ln_g"])
    put(V_LNB, inp["conv_ln_b"])
    put(V_PW2B, inp["pw2_b"])
    vecs[:, V_GQ:V_GQ + 2] = np.tile(np.asarray(inp["q_norm"], np.float32).T, (2, 1))
    vecs[:, V_GK:V_GK + 2] = np.tile(np.asarray(inp["k_norm"], np.float32).T, (2, 1))
    dw = np.asarray(inp["dw_w"], np.float32)
    dw = dw.reshape(2, CW, 8, 128)
    vecs[:, V_DWW:] = np.transpose(dw, (3, 0, 2, 1)).reshape(128, -1)
    tab = np.asarray(inp["rel_table"], np.float32)
    chb = np.ascontiguousarray(np.broadcast_to(tab[:, :, 256][None], (128, 2, NH))).astype(np.float32)
    p = np.arange(128)[:, None]
    col = np.arange(256)[None, :]
    idx = np.clip(col - p, -128, 128) + 128
    biasn = np.ascontiguousarray(np.transpose(tab[:, :, idx], (0, 2, 1, 3))).astype(np.float32)
    return vecs, chb, biasn


def kernel(**inp):
    ncores = 8
    cfg = {"NBP": 4, "SEQ": 2048, "DEPTH": 4, "SAMPLE": True}
    nc, g = build(cfg)
    vecs, chb, biasn = _host_params(inp)
    ident = np.eye(128, dtype=np.float32)
    f = lambda k: np.ascontiguousarray(np.asarray(inp[k], np.float32))
    xp, xs = f("x_prompt"), f("x_sample")
    ck, cv, sc = f("cache_k"), f("cache_v"), f("state_conv")
    shared = {k: f(k) for k in ("w_qkv", "w_o", "pw1_w", "pw2_w", "ffn_w_in", "ffn_w_out")}
    shared.update({"vecs": vecs, "chb": chb, "biasn": biasn, "ident": ident})
    in_maps = []
    for c in range(ncores):
        b0, b1 = 4 * c, 4 * c + 4
        m = dict(shared)
        m["xp"] = xp[b0:b1]
        m["xs"] = xs[b0:b1].reshape(128, D)
        m["ck"] = ck[:, b0:b1].reshape(2, 4, 512, D)
        m["cv"] = cv[:, b0:b1].reshape(2, 4, 512, D)
        m["sc"] = sc[:, b0:b1]
        in_maps.append(m)
    res = run_bass_kernel_spmd(nc, in_maps, core_ids=list(range(ncores)))
    R = res.results
    yp = np.concatenate([r["yp"] for r in R], axis=0)
    ys = np.concatenate([r["ys"].reshape(4, 32, D) for r in R], axis=0)
    kp = np.concatenate([r["kp"] for r in R], axis=1).reshape(2, 32, 512, NH, 64)
    vp = np.concatenate([r["vp"] for r in R], axis=1).reshape(2, 32, 512, NH, 64)
    ksn = np.concatenate([r["ksn"].reshape(2, 4, 32, NH, 64) for r in R], axis=1)
    vsn = np.concatenate([r["vsn"].reshape(2, 4, 32, NH, 64) for r in R], axis=1)
    cpn = np.concatenate([r["cpn"] for r in R], axis=1)
    csn = np.concatenate([r["csn"] for r in R], axis=1)
    return (yp.astype(np.float32), ys.astype(np.float32), kp.astype(np.float32), vp.astype(np.float32),
            ksn.astype(np.float32), vsn.astype(np.float32), cpn.astype(np.float32), csn.astype(np.float32))
```

```python
import contextlib
import numpy as np
import ml_dtypes
import concourse.bass as bass
import concourse.mybir as mybir
from concourse.bass_utils import run_bass_kernel_spmd

F32 = mybir.dt.float32
BF16 = mybir.dt.bfloat16
AF = mybir.ActivationFunctionType
ALU = mybir.AluOpType

D = 1024
NCH = 8
DFF = 2816
NFF = 22
NH = 16
CW = 31
EPS = 1e-6
NEG = -30000.0
ENGS = ("pe", "act", "dve", "pool", "sp")
NDMASEM = 48
NCAST = 16
NPOOLQ = 8
NXIO = 8
NSLOT = 4

V_NMIX = 0
V_NFFN = 32
V_PW1B = 64
V_DWB = 96
V_LNG = 112
V_LNB = 128
V_PW2B = 144
V_GQ = 160
V_GK = 162
V_DWW = 164
NV = V_DWW + 2 * 8 * CW


class Sched:
    def __init__(self):
        self.ops = {e: [] for e in ENGS}
        self.cnt = {e: 0 for e in ENGS}
        self.lastw = {}
        self.readers = {}
        self.waited = {e: {} for e in ENGS}
        self.dcnt = [0] * NDMASEM
        self.drr = {"a": 0, "b": 0, "c": 0, "e": 0}

    def _deps(self, eng, reads, writes):
        deps = {}

        def add(tok):
            if tok is None:
                return
            s, v = tok
            if deps.get(s, 0) < v:
                deps[s] = v

        for k in reads:
            add(self.lastw.get(k))
            if isinstance(k, tuple) and k[0] == "ps":
                rd = self.readers.get(k)
                if rd:
                    for s, v in rd.items():
                        if s != eng:
                            add((s, v))
        for k in writes:
            add(self.lastw.get(k))
            rd = self.readers.get(k)
            if rd:
                for s, v in rd.items():
                    add((s, v))
        waits = []
        wd = self.waited[eng]
        for s, v in deps.items():
            if s == eng and eng == "pe":
                continue
            if wd.get(s, 0) >= v:
                continue
            wd[s] = v
            waits.append((s, v))
        return waits

    def _commit(self, tok, reads, writes):
        s, v = tok
        for k in reads:
            rd = self.readers.setdefault(k, {})
            if rd.get(s, 0) < v:
                rd[s] = v
        for k in writes:
            self.lastw[k] = tok
            self.readers[k] = {}

    def op(self, eng, fn, reads=(), writes=(), inc=True):
        waits = self._deps(eng, reads, writes)
        if inc:
            self.cnt[eng] += 1
            tok = (eng, self.cnt[eng])
        else:
            tok = (eng, self.cnt[eng] + 1)
        self.ops[eng].append((waits, fn, eng if inc else None, 1))
        self._commit(tok, reads, writes)
        return tok

    def barrier(self):
        toks = [(e, self.cnt[e]) for e in ("pe", "act", "dve", "pool") if self.cnt[e] > 0]
        toks += [(("d", i), self.dcnt[i]) for i in range(NCAST, NDMASEM)
                 if self.dcnt[i] > 0 and not (NCAST + NPOOLQ <= i < NCAST + NPOOLQ + NXIO)]
        for e in ENGS:
            waits = []
            wd = self.waited[e]
            for sm, v in toks:
                if sm == e or wd.get(sm, 0) >= v:
                    continue
                wd[sm] = v
                waits.append((sm, v))
            if waits:
                self.ops[e].append((waits, None, None, 0))

    def dma(self, eng, fn, reads=(), writes=(), pool="b"):
        waits = self._deps(eng, reads, writes)
        if eng == "pool" and pool == "b":
            pool = "c"
        if pool == "a":
            i = self.drr["a"] % (4 if self.drr["a"] < 12 else NCAST)
        elif pool == "c":
            i = NCAST + self.drr["c"] % NPOOLQ
        elif pool == "e":
            i = NCAST + NPOOLQ + self.drr["e"] % NXIO
        else:
            i = NCAST + NPOOLQ + NXIO + self.drr["b"] % (NDMASEM - NCAST - NPOOLQ - NXIO)
        self.drr[pool] += 1
        s = ("d", i)
        wd = self.waited[eng]
        if self.dcnt[i] > 0 and wd.get(s, 0) < self.dcnt[i]:
            wd[s] = self.dcnt[i]
            waits.append((s, self.dcnt[i]))
        self.dcnt[i] += 16
        tok = (s, self.dcnt[i])
        self.ops[eng].append((waits, fn, s, 16))
        self._commit(tok, reads, writes)
        return tok


class Gen:
    def __init__(self, nc, cfg):
        self.nc = nc
        self.cfg = cfg
        self.S = Sched()
        self.psrr = {"mm": 0, "acc": 0, "all": 0}
        self.projpool = "all"
        self.rr = {}

    def bank(self, pool):
        if pool == "mm":
            b = self.psrr["mm"] % 4
        elif pool == "acc":
            b = 4 + self.psrr["acc"] % 4
        else:
            b = self.psrr["all"] % 8
        self.psrr[pool] += 1
        return b

    def rot(self, name, n):
        i = self.rr.get(name, 0)
        self.rr[name] = i + 1
        return i % n

    def mm(self, out, lhsT, rhs, start, stop, reads, writes, inc):
        self.S.op("pe", lambda e, out=out, lhsT=lhsT, rhs=rhs, start=start, stop=stop:
                  e.matmul(out, lhsT, rhs, start=start, stop=stop, skip_group_check=True),
                  reads=reads, writes=writes, inc=inc)

    def tr(self, out, in_, ident, reads, writes, inc):
        self.S.op("pe", lambda e, out=out, in_=in_, ident=ident: e.transpose(out, in_, ident),
                  reads=reads, writes=writes, inc=inc)

    def act(self, out, in_, func, reads, writes, bias=None, scale=None):
        kw = {}
        if bias is not None:
            kw["bias"] = bias
        if scale is not None:
            kw["scale"] = scale
        self.S.op("act", lambda e, out=out, in_=in_, func=func, kw=kw: e.activation(out, in_, func, **kw),
                  reads=reads, writes=writes)

    def tt(self, eng, out, in0, in1, op, reads, writes):
        self.S.op(eng, lambda e, out=out, in0=in0, in1=in1, op=op: e.tensor_tensor(out, in0, in1, op),
                  reads=reads, writes=writes)

    def ts(self, eng, out, in0, s1, s2, op0, op1, reads, writes):
        if op1 is None:
            self.S.op(eng, lambda e, out=out, in0=in0, s1=s1, op0=op0: e.tensor_scalar(out, in0, s1, None, op0),
                      reads=reads, writes=writes)
        else:
            self.S.op(eng, lambda e, out=out, in0=in0, s1=s1, s2=s2, op0=op0, op1=op1:
                      e.tensor_scalar(out, in0, s1, s2, op0, op1), reads=reads, writes=writes)

    def stt(self, eng, out, in0, sc, in1, op0, op1, reads, writes):
        self.S.op(eng, lambda e, out=out, in0=in0, sc=sc, in1=in1, op0=op0, op1=op1:
                  e.scalar_tensor_tensor(out, in0, sc, in1, op0, op1), reads=reads, writes=writes)

    def rsqrt(self, out, in_, reads, wkey):
        self.act(out, in_, AF.Ln, reads=list(reads) + ["const"], writes=[wkey], bias=self.epsc[:, 0:1])
        self.act(out, out, AF.Exp, reads=[wkey], writes=[wkey], scale=-0.5)

    def cp(self, eng, out, in_, reads, writes):
        self.S.op(eng, lambda e, out=out, in_=in_: e.tensor_copy(out, in_), reads=reads, writes=writes)

    def memset(self, eng, ap, val, writes):
        self.S.op(eng, lambda e, ap=ap, val=val: e.memset(ap, val), writes=writes)

    def dma(self, eng, out, in_, reads, writes, pool="b"):
        return self.S.dma(eng, lambda e, out=out, in_=in_: e.dma_start(out=out, in_=in_), reads=reads, writes=writes,
                          pool=pool)

    def plan_weights(self):
        self.tid = {}
        n = 0
        for L in range(self.cfg["DEPTH"]):
            if L % 2 == 0:
                names = ["q0", "q1", "k0", "k1", "v0", "v1", "o0", "o1"]
            else:
                names = ["p0", "p1", "p2", "p3", "w0", "w1"]
            names += ["i%d" % i for i in range(11)] + ["f%d" % i for i in range(8)]
            for nm in names:
                self.tid[(L, nm)] = n
                n += 1
        self.ntiles = n

    def emit_casts(self):
        g = self
        W = self.W
        wscr = self.wscr

        def cast(tid, dst, src):
            g.dma("pool", dst, src, reads=(), writes=[("wscr", tid)], pool="a")

        for L in range(self.cfg["DEPTH"]):
            a = L // 2
            if L % 2 == 0:
                for i, nm in enumerate(["q0", "q1", "k0", "k1", "v0", "v1"]):
                    tid = self.tid[(L, nm)]
                    src = W["w_qkv"][a, :, i * 512:(i + 1) * 512].rearrange("(kc p) n -> p kc n", p=128)
                    cast(tid, wscr[tid].rearrange("p (kc n) -> p kc n", kc=8), src)
                for i, nm in enumerate(["o0", "o1"]):
                    tid = self.tid[(L, nm)]
                    src = W["w_o"][a, :, i * 512:(i + 1) * 512].rearrange("(kc p) n -> p kc n", p=128)
                    cast(tid, wscr[tid].rearrange("p (kc n) -> p kc n", kc=8), src)
            else:
                for pi in range(4):
                    tid = self.tid[(L, "p%d" % pi)]
                    dst4 = wscr[tid].rearrange("p (kc q n) -> p kc q n", kc=8, q=4)
                    for jj in range(2):
                        j = 2 * pi + jj
                        for gsel in range(2):
                            c0 = gsel * D + j * 128
                            src = W["pw1_w"][a, :, c0:c0 + 128].rearrange("(kc p) n -> p kc n", p=128)
                            cast(tid, dst4[:, :, 2 * jj + gsel, :], src)
                for i, nm in enumerate(["w0", "w1"]):
                    tid = self.tid[(L, nm)]
                    src = W["pw2_w"][a, :, i * 512:(i + 1) * 512].rearrange("(kc p) n -> p kc n", p=128)
                    cast(tid, wscr[tid].rearrange("p (kc n) -> p kc n", kc=8), src)
            for ii in range(11):
                tid = self.tid[(L, "i%d" % ii)]
                dst4 = wscr[tid].rearrange("p (kc q n) -> p kc q n", kc=8, q=4)
                for jj in range(2):
                    j = 2 * ii + jj
                    for gsel in range(2):
                        c0 = gsel * DFF + j * 128
                        src = W["ffn_w_in"][L, :, c0:c0 + 128].rearrange("(kc p) n -> p kc n", p=128)
                        cast(tid, dst4[:, :, 2 * jj + gsel, :], src)
            for o in range(8):
                tid = self.tid[(L, "f%d" % o)]
                src = W["ffn_w_out"][L, :, o * 128:(o + 1) * 128].rearrange("(kc p) n -> p kc n", p=128)
                cast(tid, wscr[tid][:, 0:NFF * 128].rearrange("p (kc n) -> p kc n", kc=NFF), src)

    def wnext(self, L, nm):
        i = self.wpos
        assert self.wseq[i] == (L, nm), (i, self.wseq[i], (L, nm))
        self.wpos += 1
        while self.wissued < min(len(self.wseq), i + NSLOT - 1):
            k = self.wissued
            tid = self.tid[self.wseq[k]]
            s = k % NSLOT
            ncol = NFF * 128 if self.wseq[k][1].startswith("f") else 4096
            self.dma("sp", self.wslots[:, s, 0:ncol], self.wscr[tid][:, 0:ncol], reads=[("wscr", tid)], writes=[("ws", s)])
            self.wissued += 1
        s = i % NSLOT
        return self.wslots[:, s, :], ("ws", s)

    def rmsnorm(self, xap, xkeys, gbase, W, hdst, hkeys, tcols, sq, sqkeys, hextra=()):
        self.act(sq[:, :, 0:W], xap, AF.Square, reads=xkeys, writes=sqkeys)
        b = self.bank("mm")
        for j in range(NCH):
            self.mm(self.ps[:, b, 0:W], self.meanmat[:], sq[:, j, 0:W], j == 0, j == NCH - 1,
                    reads=sqkeys + ["const"], writes=[("ps", b)], inc=(j == NCH - 1))
        ri = 0
        rstd = self.rstd[:, 0:W]
        self.rsqrt(rstd, self.ps[:, b, 0:W], reads=[("ps", b)], wkey=("rstd", ri))
        for j in range(NCH):
            eng = "dve"
            self.stt(eng, hdst[:, j, tcols], xap[:, j, :], self.vecs[:, gbase + j:gbase + j + 1], rstd,
                     ALU.mult, ALU.mult, reads=[xkeys[j], ("rstd", ri), "const"], writes=[hkeys[j]] + list(hextra))

    def proj_feat(self, wslot_ap, wkey, csel, src, srckeys, W, nk=NCH):
        b = self.bank(self.projpool)
        for kc in range(nk):
            self.mm(self.ps[:, b, 0:W], csel(kc), src(kc), kc == 0, kc == nk - 1,
                    reads=[wkey] + srckeys, writes=[("ps", b)], inc=(kc == nk - 1))
        return b

    def load_x_seq(self, s, tts=None):
        for tt in (range(self.cfg["SEQ"] // 128) if tts is None else tts):
            xi = self.rot("xin", 2)
            self.dma("sp", self.xin[:, xi, :], self.xp[s, tt * 128:(tt + 1) * 128, :], reads=[], writes=[("xin", xi)],
                     pool="e")
            for g4 in range(2):
                b = self.bank("mm")
                for q in range(4):
                    j = g4 * 4 + q
                    self.tr(self.ps[:, b, q * 128:(q + 1) * 128], self.xin[:, xi, j * 128:(j + 1) * 128], self.identf[:],
                            reads=[("xin", xi), "const"], writes=[("ps", b)], inc=(q == 3))
                t = tt // 4
                eng = "dve" if g4 == 0 else "act"
                out = self.xT[:, g4 * 4:(g4 + 1) * 4, tt * 128:(tt + 1) * 128]
                in_ = self.ps[:, b, :].rearrange("p (q n) -> p q n", q=4)
                wk = [("x", j, t) for j in range(g4 * 4, g4 * 4 + 4)]
                if eng == "dve":
                    self.cp("dve", out, in_, reads=[("ps", b)], writes=wk)
                else:
                    self.act(out, in_, AF.Copy, reads=[("ps", b)], writes=wk)

    def store_y_seq(self, s, tts=None):
        for tt in (range(self.cfg["SEQ"] // 128) if tts is None else tts):
            t = tt // 4
            for g4 in range(2):
                b = self.bank("mm")
                for q in range(4):
                    j = g4 * 4 + q
                    self.tr(self.ps[:, b, q * 128:(q + 1) * 128], self.xT[:, j, tt * 128:(tt + 1) * 128], self.identf[:],
                            reads=[("x", j, t), "const"], writes=[("ps", b)], inc=(q == 3))
                oi = self.rot("ost", 3)
                eng = "dve" if g4 == 0 else "act"
                if eng == "dve":
                    self.cp("dve", self.ost[:, oi, :], self.ps[:, b, :], reads=[("ps", b)], writes=[("ost", oi)])
                else:
                    self.act(self.ost[:, oi, :], self.ps[:, b, :], AF.Copy, reads=[("ps", b)], writes=[("ost", oi)])
                self.dma("sp", self.yp[s, tt * 128:(tt + 1) * 128, g4 * 512:(g4 + 1) * 512], self.ost[:, oi, :],
                         reads=[("ost", oi)], writes=[], pool="e")

    def qkv_tile(self, L, s, t, sample=False):
        a = L // 2
        W = 128 if sample else 512
        h = self.h
        hk = [("h", j) for j in range(NCH)]
        last = (not sample) and (t == self.cfg["SEQ"] // 512 - 1) and "nokout" not in self.cfg.get("SKIP", ())
        for which in ("q", "k"):
            for c in range(NCH):
                if c % 4 == 0:
                    wsl, wkey = self.wnext(L, "%s%d" % (which, c // 4))
                    w3 = wsl.rearrange("p (kc n) -> p kc n", kc=8)
                cc = c % 4
                b = self.proj_feat(wsl, wkey, lambda kc, w3=w3, cc=cc: w3[:, kc, cc * 128:(cc + 1) * 128],
                                   lambda kc: h[:, kc, 0:W], hk, W)
                si = self.rot("sqh", 2)
                self.act(self.sqh[:, si, 0:W], self.ps[:, b, 0:W], AF.Square, reads=[("ps", b)], writes=[("sqh", si)])
                b2 = self.bank(self.projpool)
                self.mm(self.ps[:, b2, 0:W], self.blkmean[:], self.sqh[:, si, 0:W], True, True,
                        reads=[("sqh", si), "const"], writes=[("ps", b2)], inc=True)
                ri = self.rot("rs", 2)
                self.rsqrt(self.rs[:, ri, 0:W], self.ps[:, b2, 0:W], reads=[("ps", b2)], wkey=("rs", ri))
                if which == "q":
                    for par in range(2):
                        p0 = 64 * par
                        self.stt("dve", self.qT[p0:p0 + 64, c, par, 0:W], self.ps[p0:p0 + 64, b, 0:W],
                                 self.gq8[p0:p0 + 64, a:a + 1], self.rs[p0:p0 + 64, ri, 0:W], ALU.mult, ALU.mult,
                                 reads=[("ps", b), ("rs", ri), "const"], writes=[("q", c)])
                else:
                    gcol = self.vecs[:, V_GK + a:V_GK + a + 1]
                    if sample:
                        kdst = self.kTs[:, c, 0:W]
                        kkey = ("kTs", c)
                    else:
                        ks = (t % 2) * 512
                        kdst = self.kT[:, c, ks:ks + 512]
                        kkey = ("kT", c, t % 2)
                    if last or sample:
                        fi = self.rot("kf", 1)
                        kf = self.kf[:, fi, 0:W]
                        self.stt("dve", kf, self.ps[:, b, 0:W], gcol, self.rs[:, ri, 0:W], ALU.mult, ALU.mult,
                                 reads=[("ps", b), ("rs", ri), "const"], writes=[("kf", fi)])
                        self.act(kdst, kf, AF.Copy, reads=[("kf", fi)], writes=[kkey])
                        nsub = W // 128 if "nokt" not in self.cfg.get("SKIP", ()) else 0
                        b3 = self.bank(self.projpool)
                        for sub in range(nsub):
                            self.tr(self.ps[:, b3, sub * 128:(sub + 1) * 128], self.kf[:, fi, sub * 128:(sub + 1) * 128],
                                    self.identf[:], reads=[("kf", fi), "const"], writes=[("ps", b3)], inc=(sub == nsub - 1))
                        oi = self.rot("ost", 3)
                        if nsub:
                            self.act(self.ost[:, oi, 0:W], self.ps[:, b3, 0:W], AF.Copy, reads=[("ps", b3)], writes=[("ost", oi)])
                        if "nokdma" in self.cfg.get("SKIP", ()):
                            pass
                        elif sample:
                            dst = self.ksn[a, :, c * 128:(c + 1) * 128]
                            self.dma("sp", dst, self.ost[:, oi, 0:128], reads=[("ost", oi)], writes=[])
                        else:
                            dst = self.kp[a, s, :, c * 128:(c + 1) * 128].rearrange("(sub p) n -> p sub n", p=128)
                            self.dma("sp", dst, self.ost[:, oi, :].rearrange("p (sub n) -> p sub n", sub=4),
                                     reads=[("ost", oi)], writes=[])
                    else:
                        self.stt("dve", kdst, self.ps[:, b, 0:W], gcol, self.rs[:, ri, 0:W], ALU.mult, ALU.mult,
                                 reads=[("ps", b), ("rs", ri), "const"], writes=[kkey])
        wv = []
        for hv in range(2):
            wsl, wkey = self.wnext(L, "v%d" % hv)
            wv.append((wsl.rearrange("p (kc n) -> p kc n", kc=8), wkey))
        nsub = 4
        TP = 32 if sample else 128
        for sub in range(nsub):
            for hv in range(2):
                w3, wkey = wv[hv]
                b = self.bank("mm")
                for kc in range(NCH):
                    self.mm(self.ps[0:TP, b, :], h[:, kc, sub * TP:(sub + 1) * TP], w3[:, kc, :], kc == 0, kc == NCH - 1,
                            reads=[wkey] + hk, writes=[("ps", b)], inc=(kc == NCH - 1))
                if sample:
                    vdst = self.Vs[0:TP, sub, 4 * hv:4 * hv + 4, :]
                    vkey = ("Vs", hv)
                else:
                    vs = (4 * t + sub) % 8
                    vdst = self.V[:, vs, 4 * hv:4 * hv + 4, :]
                    vkey = ("V", vs, hv)
                vdst = vdst.rearrange("p i (e n) -> p i e n", e=3)[:, :, 0:3:2, :]
                src = self.ps[0:TP, b, :].rearrange("p (i e n) -> p i e n", i=4, e=2)
                if hv == 0:
                    self.cp("dve", vdst, src, reads=[("ps", b)], writes=[vkey])
                else:
                    self.act(vdst, src, AF.Copy, reads=[("ps", b)], writes=[vkey])
                if (last or sample) and "novout" not in self.cfg.get("SKIP", ()):
                    oi = self.rot("ost", 3)
                    if hv == 0:
                        self.cp("dve", self.ost[0:TP, oi, :], self.ps[0:TP, b, :], reads=[("ps", b)], writes=[("ost", oi)])
                    else:
                        self.act(self.ost[0:TP, oi, :], self.ps[0:TP, b, :], AF.Copy, reads=[("ps", b)], writes=[("ost", oi)])
                    if sample:
                        dst = self.vsn[a, sub * 32:(sub + 1) * 32, hv * 512:(hv + 1) * 512]
                    else:
                        dst = self.vp[a, s, sub * 128:(sub + 1) * 128, hv * 512:(hv + 1) * 512]
                    if "novdma" not in self.cfg.get("SKIP", ()):
                        self.dma("sp", dst, self.ost[0:TP, oi, :], reads=[("ost", oi)], writes=[])

    def attn_tile(self, L, t):
        a = L // 2
        W = 512
        steps = []
        for i in range(8):
            ms = [4 * t] + [m for m in (4 * t - 1, 4 * t + 1, 4 * t - 2, 4 * t + 2, 4 * t - 3, 4 * t + 3, 4 * t - 4) if m >= 0]
            for idx, m in enumerate(ms):
                steps.append((i, idx, m, idx == len(ms) - 1))
        accs = {}
        pend = None

        def qk(step):
            i, idx, m, lastm = step
            r0 = max(0, 8 * t - 2 * m)
            r1 = min(10, 8 * t - 2 * m + 8)
            w = (r1 - r0) * 64
            c0 = (2 * m + r0 - 8 * t) * 64
            kslot = (m // 4) % 2
            kc0 = kslot * 512 + (m % 4) * 128
            b0 = 2 * self.rot("stpair", 2)
            pi0 = 2 * self.rot("ptpair", 2)
            rn = min(r1, 4)
            for par in range(2):
                hd = 2 * i + par
                p0 = 64 * par
                b = b0 + par
                extra = []
                if r1 == 10:
                    extra.append("mask")
                if r0 < rn:
                    extra.append("bias")
                lastpair = (par == 1)
                self.mm(self.ps[:, b, 0:w], self.kT[:, i, kc0:kc0 + 128], self.qT[:, i, par, c0:c0 + w],
                        True, not extra, reads=[("kT", i, kslot), ("q", i)], writes=[("ps", b)],
                        inc=(lastpair and not extra))
                for ei, ex in enumerate(extra):
                    fin = (ei == len(extra) - 1)
                    if ex == "mask":
                        self.mm(self.ps[:, b, w - 64:w], self.identb[:], self.maskb[:], False, fin,
                                reads=["const"], writes=[("ps", b)], inc=(lastpair and fin))
                    else:
                        wn = (rn - r0) * 64
                        self.mm(self.ps[:, b, 0:wn], self.identb[:], self.biasb[:, hd, r0 * 64:rn * 64], False, fin,
                                reads=["const", ("biasn", a)], writes=[("ps", b)], inc=(lastpair and fin))
            self.act(self.PT[:, pi0:pi0 + 2, 0:w], self.ps[:, b0:b0 + 2, 0:w], AF.Exp,
                     reads=[("ps", b0), ("ps", b0 + 1)], writes=[("PT", pi0), ("PT", pi0 + 1)])
            return [(pi0, w, c0), (pi0 + 1, w, c0)]

        def pv(step, res):
            i, idx, m, lastm = step
            vs = m % 8
            if idx == 0:
                accs[i] = (self.bank("acc"), self.bank("acc"))
            for par in range(2):
                pi, w, c0 = res[par]
                b = accs[i][par]
                lo = 64 * par
                self.mm(self.ps[:, b, c0:c0 + w], self.V[:, vs, i, lo:lo + 128], self.PT[:, pi, 0:w], idx == 0, lastm,
                        reads=[("PT", pi), ("V", vs, i // 4)], writes=[("ps", b)], inc=True)
            if lastm and "nofin" not in self.cfg.get("SKIP", ()):
                be, bo = accs[i]
                ri = self.rot("rec", 1)
                rec = self.rec[:, ri, :]
                self.act(rec[0:64, :], self.ps[64:128, be, :], AF.Copy, reads=[("ps", be)], writes=[("rec", ri, 0)])
                self.act(rec[64:128, :], self.ps[0:64, bo, :], AF.Copy, reads=[("ps", bo)], writes=[("rec", ri, 1)])
                self.S.op("dve", lambda e, o=rec: e.reciprocal(o, o),
                          reads=[("rec", ri, 0), ("rec", ri, 1)], writes=[("rec", ri, 0), ("rec", ri, 1)])
                xk = [("xin", 0), ("xin", 1)]
                self.tt("dve", self.oT[0:64, i, 0:W], self.ps[0:64, be, :], rec[0:64, :], ALU.mult,
                        reads=[("ps", be), ("rec", ri, 0)], writes=[("oT", i)] + xk)
                self.tt("dve", self.oT[64:128, i, 0:W], self.ps[64:128, bo, :], rec[64:128, :], ALU.mult,
                        reads=[("ps", bo), ("rec", ri, 1)], writes=[("oT", i)] + xk)

        prev = None
        for st in steps:
            res = qk(st)
            if prev is not None:
                pv(*prev)
            prev = (st, res)
        pv(*prev)

    def out_proj(self, L, t, names, xsel, xkeys, W, bcol=None, src=None, srck=None):
        h = self.h if src is None else src
        hk = [("h", j) for j in range(NCH)] if srck is None else srck
        for c in range(NCH):
            if c % 4 == 0:
                wsl, wkey = self.wnext(L, names[c // 4])
                w3 = wsl.rearrange("p (kc n) -> p kc n", kc=8)
            cc = c % 4
            b = self.proj_feat(wsl, wkey, lambda kc, w3=w3, cc=cc: w3[:, kc, cc * 128:(cc + 1) * 128],
                               lambda kc: h[:, kc, 0:W], hk, W)
            if bcol is None:
                self.tt("dve", xsel(c), self.ps[:, b, 0:W], xsel(c), ALU.add, reads=[("ps", b), xkeys[c]], writes=[xkeys[c]])
            else:
                self.stt("dve", xsel(c), self.ps[:, b, 0:W], self.vecs[:, bcol + c:bcol + c + 1], xsel(c), ALU.add, ALU.add,
                         reads=[("ps", b), xkeys[c], "const"], writes=[xkeys[c]])

    def ffn_norm(self, L, tiles, W, h_alias=(), sq_alias=()):
        h2k = [("h2", j) for j in range(NCH)]
        for ti, (xap, xkeys, xsel) in enumerate(tiles):
            self.rmsnorm(xap, xkeys, V_NFFN + L * 8, W, self.h2, h2k, slice(ti * W, (ti + 1) * W),
                         self.sqF, [("sqF", j) for j in range(NCH)] + list(sq_alias), hextra=h_alias)

    def ffn(self, L, tiles, W):
        self.ffn_norm(L, tiles, W)
        self.ffn_p1(L, tiles, W)
        self.ffn_p2(L, tiles, W)

    def ffn_p1(self, L, tiles, W):
        nt = len(tiles)
        h2 = self.h2
        h2k = [("h2", j) for j in range(NCH)]
        for ii in range(11):
            wsl, wkey = self.wnext(L, "i%d" % ii)
            w4 = wsl.rearrange("p (kc q n) -> p kc q n", kc=8, q=4)
            for jj in range(2):
                j = 2 * ii + jj
                for ti in range(nt):
                    bg = self.bank("all")
                    for kc in range(NCH):
                        self.mm(self.ps[:, bg, 0:W], w4[:, kc, 2 * jj, :], h2[:, kc, ti * W:(ti + 1) * W], kc == 0, kc == NCH - 1,
                                reads=[wkey] + h2k, writes=[("ps", bg)], inc=(kc == NCH - 1))
                    bu = self.bank("all")
                    for kc in range(NCH):
                        self.mm(self.ps[:, bu, 0:W], w4[:, kc, 2 * jj + 1, :], h2[:, kc, ti * W:(ti + 1) * W], kc == 0, kc == NCH - 1,
                                reads=[wkey] + h2k, writes=[("ps", bu)], inc=(kc == NCH - 1))
                    si = self.rot("sg", 3)
                    self.act(self.sg[:, si, 0:W], self.ps[:, bg, 0:W], AF.Silu, reads=[("ps", bg)], writes=[("sg", si)])
                    self.tt("dve", self.aT[:, j, ti * W:(ti + 1) * W], self.sg[:, si, 0:W], self.ps[:, bu, 0:W], ALU.mult,
                            reads=[("sg", si), ("ps", bu)], writes=[("aT", j)])
    def ffn_p2(self, L, tiles, W):
        aTk = [("aT", j) for j in range(NFF)]
        for o in range(NCH):
            wsl, wkey = self.wnext(L, "f%d" % o)
            w3 = wsl[:, 0:NFF * 128].rearrange("p (kc n) -> p kc n", kc=NFF)
            for ti, (xap, xkeys, xsel) in enumerate(tiles):
                b = self.bank("all")
                for kc in range(NFF):
                    self.mm(self.ps[:, b, 0:W], w3[:, kc, :], self.aT[:, kc, ti * W:(ti + 1) * W], kc == 0, kc == NFF - 1,
                            reads=[wkey] + aTk, writes=[("ps", b)], inc=(kc == NFF - 1))
                self.tt("dve", xsel(o), self.ps[:, b, 0:W], xsel(o), ALU.add, reads=[("ps", b), xkeys[o]], writes=[xkeys[o]])

    def conv_tile(self, L, s, t, xap, xkeys, xsel, sample=False):
        c = L // 2
        W = 128 if sample else 512
        h = self.h
        hk = [("h", j) for j in range(NCH)]
        last = (not sample) and (t == self.cfg["SEQ"] // 512 - 1)
        u = self.u
        UW = 248 if sample else 30 + 512
        sqc = self.y[:, 0:4, :].bitcast(BF16).rearrange("p a (b n) -> p (a b) n", b=2)
        self.rmsnorm(xap, xkeys, V_NMIX + L * 8, W, h, hk, slice(0, W), sqc, [("y", j) for j in range(4)])
        yield "R"
        if sample:
            pass
        elif t == 0:
            self.memset("dve", u[:, :, 0:30], 0.0, writes=[("u", j) for j in range(NCH)])
        else:
            self.cp("dve", u[:, :, 0:30], u[:, :, 512:542], reads=[("u", j) for j in range(NCH)],
                    writes=[("u", j) for j in range(NCH)])
        for pi in range(4):
            wsl, wkey = self.wnext(L, "p%d" % pi)
            w4 = wsl.rearrange("p (kc q n) -> p kc q n", kc=8, q=4)
            for jj in range(2):
                j = 2 * pi + jj
                ba = self.proj_feat(wsl, wkey, lambda kc, w4=w4, jj=jj: w4[:, kc, 2 * jj, :], lambda kc: h[:, kc, 0:W], hk, W)
                bg = self.proj_feat(wsl, wkey, lambda kc, w4=w4, jj=jj: w4[:, kc, 2 * jj + 1, :], lambda kc: h[:, kc, 0:W], hk, W)
                si = self.rot("sig", 2)
                bcol_a = V_PW1B + c * 16 + j
                bcol_g = V_PW1B + c * 16 + 8 + j
                self.act(self.sig[:, si, 0:W], self.ps[:, bg, 0:W], AF.Sigmoid, reads=[("ps", bg), "const"],
                         writes=[("sig", si)], bias=self.vecs[:, bcol_g:bcol_g + 1])
                if sample:
                    udst = u[:, j, 0:248].rearrange("p (b n) -> p b n", b=4)[:, :, 30:62]
                    psa = self.ps[:, ba, 0:W].rearrange("p (b n) -> p b n", b=4)
                    sg_ = self.sig[:, si, 0:W].rearrange("p (b n) -> p b n", b=4)
                else:
                    udst = u[:, j, 30:30 + W]
                    psa = self.ps[:, ba, 0:W]
                    sg_ = self.sig[:, si, 0:W]
                self.stt("dve", udst, psa, self.vecs[:, bcol_a:bcol_a + 1], sg_, ALU.add, ALU.mult,
                         reads=[("ps", ba), ("sig", si), "const"], writes=[("u", j)])
                if last or sample:
                    self.stt("dve", self.uf[:, j, 0:W if sample else 32],
                             self.ps[:, ba, 0:W] if sample else self.ps[:, ba, W - 32:W],
                             self.vecs[:, bcol_a:bcol_a + 1],
                             self.sig[:, si, 0:W] if sample else self.sig[:, si, W - 32:W], ALU.add, ALU.mult,
                             reads=[("ps", ba), ("sig", si), "const"], writes=[("uf", j)])
        if last or sample:
            nb = 4 if sample else 1
            for bb in range(nb):
                for g4 in range(2):
                    b = self.bank("mm")
                    for q in range(4):
                        j = g4 * 4 + q
                        self.tr(self.ps[0:32, b, q * 128:(q + 1) * 128], self.uf[:, j, bb * 32:(bb + 1) * 32], self.identf[:],
                                reads=[("uf", j), "const"], writes=[("ps", b)], inc=(q == 3))
                    oi = self.rot("ost", 3)
                    self.act(self.ost[0:32, oi, :], self.ps[0:32, b, :], AF.Copy, reads=[("ps", b)], writes=[("ost", oi)])
                    if sample:
                        dst = self.csn[c, bb, :, g4 * 512:(g4 + 1) * 512]
                    else:
                        dst = self.cpn[c, s, :, g4 * 512:(g4 + 1) * 512]
                    self.dma("sp", dst, self.ost[2:32, oi, :], reads=[("ost", oi)], writes=[])
        yield "P"
        NO = (UW - 30) if sample else W
        for j in range(NCH):
            di = self.rot("dg", 2)
            col = V_DWW + (c * 8 + j) * CW
            ib = self.identb[:]
            vb = self.vecs[:, col:col + CW]
            in0 = bass.AP(ib.tensor, ib.offset, [list(ib.ap[0]), [0, CW], list(ib.ap[1])])
            in1 = bass.AP(vb.tensor, vb.offset, [list(vb.ap[0]), list(vb.ap[1]), [0, 128]])
            self.tt("dve", self.dg[:, di, :, :], in0, in1, ALU.mult, reads=["const"], writes=[("dg", di)])
            b = self.bank("acc")
            for w in range(CW):
                self.mm(self.ps[:, b, 0:NO], self.dg[:, di, w, :], u[:, j, w:w + NO], w == 0, w == CW - 1,
                        reads=[("dg", di), ("u", j)], writes=[("ps", b)], inc=(w == CW - 1))
            bcol = V_DWB + c * 8 + j
            if sample:
                src = self.ps[:, b, 0:248].rearrange("p (b n) -> p b n", b=4)[:, :, 0:32]
                ydst = self.y[:, j, 0:W].rearrange("p (b n) -> p b n", b=4)
                ybd = self.ybf[:, j, 0:W].rearrange("p (b n) -> p b n", b=4)
                ysd = self.ysq[:, j, 0:W].rearrange("p (b n) -> p b n", b=4)
            else:
                src = self.ps[:, b, 0:W]
                ydst = self.y[:, j, 0:W]
                ybd = self.ybf[:, j, 0:W]
                ysd = self.ysq[:, j, 0:W]
            self.act(ydst, src, AF.Identity, reads=[("ps", b), "const"], writes=[("y", j)], bias=self.vecs[:, bcol:bcol + 1])
            self.act(ybd, ydst, AF.Copy, reads=[("y", j)], writes=[("ybf", j)])
            self.act(ysd, ydst, AF.Square, reads=[("y", j)], writes=[("ysq", j)])
        b1 = self.bank("mm")
        for j in range(NCH):
            self.mm(self.ps[:, b1, 0:W], self.meanmat[:], self.ybf[:, j, 0:W], j == 0, j == NCH - 1,
                    reads=[("ybf", j), "const"], writes=[("ps", b1)], inc=(j == NCH - 1))
        b2 = self.bank("mm")
        for j in range(NCH):
            self.mm(self.ps[:, b2, 0:W], self.meanmat[:], self.ysq[:, j, 0:W], j == 0, j == NCH - 1,
                    reads=[("ysq", j), "const"], writes=[("ps", b2)], inc=(j == NCH - 1))
        mean = self.st[:, 0, 0:W]
        msq = self.st[:, 1, 0:W]
        rstd = self.st[:, 2, 0:W]
        self.cp("dve", mean, self.ps[:, b1, 0:W], reads=[("ps", b1)], writes=[("st", 0)])
        self.tt("dve", msq, mean, mean, ALU.mult, reads=[("st", 0)], writes=[("st", 1)])
        self.tt("dve", rstd, self.ps[:, b2, 0:W], msq, ALU.subtract, reads=[("ps", b2), ("st", 1)], writes=[("st", 2)])
        self.rsqrt(rstd, rstd, reads=[("st", 2)], wkey=("st", 2))
        yield "B"
        ybk = [("ybf", j) for j in range(NCH)]
        for j in range(NCH):
            yj = self.y[:, j, 0:W]
            eng = "dve"
            self.tt(eng, yj, yj, mean, ALU.subtract, reads=[("y", j), ("st", 0)], writes=[("y", j)])
            self.tt(eng, yj, yj, rstd, ALU.mult, reads=[("y", j), ("st", 2)], writes=[("y", j)])
            gcol = V_LNG + c * 8 + j
            bcol = V_LNB + c * 8 + j
            self.act(self.ybf[:, j, 0:W], yj, AF.Silu, reads=[("y", j), "const"], writes=[("ybf", j)],
                     bias=self.vecs[:, bcol:bcol + 1], scale=self.vecs[:, gcol:gcol + 1])
        yield "C1"
        self.out_proj(L, t, ["w0", "w1"], xsel, xkeys, W, bcol=V_PW2B + c * 8, src=self.ybf, srck=ybk)
        yield "C2"

    def sample_cache_load(self, k):
        if k >= len(self.sjobs):
            return
        L, bb = self.sjobs[k]
        a = L // 2
        buf = k % 2
        self.dma("sp", self.ck32[buf], self.ck[a, bb].rearrange("(m p) n -> p m n", p=128), reads=[], writes=[("ck32", buf)])
        self.dma("sp", self.cv32[buf], self.cv[a, bb].rearrange("(m p) n -> p m n", p=128), reads=[], writes=[("cv32", buf)])

    def attn_sample(self, L):
        a = L // 2
        for bb in range(4):
            k = self.sjobs.index((L, bb))
            buf = k % 2
            if k == 0:
                self.sample_cache_load(0)
            self.sample_cache_load(k + 1)
            for m in range(4):
                for g4 in range(2):
                    b = self.bank("mm")
                    for q in range(4):
                        j = g4 * 4 + q
                        self.tr(self.ps[:, b, q * 128:(q + 1) * 128], self.ck32[buf][:, m, j * 128:(j + 1) * 128], self.identf[:],
                                reads=[("ck32", buf), "const"], writes=[("ps", b)], inc=(q == 3))
                    dst = self.kc[:, g4 * 4:g4 * 4 + 4, m * 128:(m + 1) * 128]
                    src = self.ps[:, b, :].rearrange("p (q n) -> p q n", q=4)
                    if g4 == 0:
                        self.cp("dve", dst, src, reads=[("ps", b)], writes=[("kc", m)])
                    else:
                        self.act(dst, src, AF.Copy, reads=[("ps", b)], writes=[("kc", m)])
                vsrc = self.cv32[buf][:, m, :].rearrange("p (i e n) -> p i e n", i=8, e=2)
                vdst = self.Vc[:, m, :, :].rearrange("p i (e n) -> p i e n", e=3)[:, :, 0:3:2, :]
                if m % 2 == 0:
                    self.cp("dve", vdst, vsrc, reads=[("cv32", buf)], writes=["Vc"])
                else:
                    self.act(vdst, vsrc, AF.Copy, reads=[("cv32", buf)], writes=["Vc"])
            q0 = bb * 32
            for i in range(8):
                res = []
                for par in range(2):
                    hd = 2 * i + par
                    p0 = 64 * par
                    b = self.bank("mm")
                    kcr = [("kc", mm_) for mm_ in range(4)] + [("q", i)]
                    self.mm(self.ps[:, b, 96:128], self.kc[p0:p0 + 64, i, 384:512], self.qT[p0:p0 + 64, i, par, q0:q0 + 32],
                            True, False, reads=kcr, writes=[("ps", b)], inc=False)
                    self.mm(self.ps[:, b, 96:128], self.identb[:], self.biasb[:, hd, 128:160], False, True,
                            reads=["const", ("biasn", a)], writes=[("ps", b)], inc=False)
                    self.mm(self.ps[0:32, b, 128:160], self.kTs[p0:p0 + 64, i, q0:q0 + 32], self.qT[p0:p0 + 64, i, par, q0:q0 + 32],
                            True, False, reads=[("kTs", i), ("q", i)], writes=[("ps", b)], inc=False)
                    self.mm(self.ps[0:32, b, 128:160], self.identb[0:32, 0:32], self.biasb[0:32, hd, 0:32], False, True,
                            reads=["const", ("biasn", a)], writes=[("ps", b)], inc=False)
                    for m in range(3):
                        self.mm(self.ps[:, b, m * 32:(m + 1) * 32], self.kc[p0:p0 + 64, i, m * 128:(m + 1) * 128],
                                self.qT[p0:p0 + 64, i, par, q0:q0 + 32], True, True, reads=kcr, writes=[("ps", b)], inc=(m == 2))
                    pi = self.rot("PT", 4)
                    self.act(self.PT[:, pi, 0:128], self.ps[:, b, 0:128], AF.Exp, reads=[("ps", b)], writes=[("PT", pi)])
                    self.act(self.PT[0:32, pi, 128:160], self.ps[0:32, b, 128:160], AF.Exp, reads=[("ps", b)],
                             writes=[("PT", pi)])
                    res.append(pi)
                accb = (self.bank("acc"), self.bank("acc"))
                for par in range(2):
                    pi = res[par]
                    b = accb[par]
                    lo = 64 * par
                    for m in range(4):
                        self.mm(self.ps[:, b, 0:32], self.Vc[:, m, i, lo:lo + 128], self.PT[:, pi, m * 32:(m + 1) * 32],
                                m == 0, False, reads=[("PT", pi), "Vc"], writes=[("ps", b)], inc=False)
                    self.mm(self.ps[:, b, 0:32], self.Vs[0:32, bb, i, lo:lo + 128], self.PT[0:32, pi, 128:160],
                            False, True, reads=[("PT", pi), ("Vs", i // 4)], writes=[("ps", b)], inc=True)
                be, bo = accb
                ri = self.rot("rec", 1)
                rec = self.rec[:, ri, 0:32]
                self.S.op("dve", lambda e, o=rec[0:64, :], x=self.ps[64:128, be, 0:32]: e.reciprocal(o, x),
                          reads=[("ps", be)], writes=[("rec", ri, 0)])
                self.S.op("dve", lambda e, o=rec[64:128, :], x=self.ps[0:64, bo, 0:32]: e.reciprocal(o, x),
                          reads=[("ps", bo)], writes=[("rec", ri, 1)])
                self.tt("dve", self.oT[0:64, i, q0:q0 + 32], self.ps[0:64, be, 0:32], rec[0:64, :], ALU.mult,
                        reads=[("ps", be), ("rec", ri, 0)], writes=[("oT", i)])
                self.tt("dve", self.oT[64:128, i, q0:q0 + 32], self.ps[64:128, bo, 0:32], rec[64:128, :], ALU.mult,
                        reads=[("ps", bo), ("rec", ri, 1)], writes=[("oT", i)])

    def load_bias(self, a, stkeys):
        bk = [("biasn", 0), ("biasn", 1)]
        self.dma("sp", self.biasst[:], self.biasn_d[a], reads=[], writes=stkeys)
        cb = self.chb[:, a, :]
        cbb = bass.AP(cb.tensor, cb.offset, [list(cb.ap[0]), list(cb.ap[1]), [0, 256]])
        self.tt("dve", self.biasb[:], self.biasst[:], cbb, ALU.subtract, reads=stkeys + ["const"], writes=bk)
        self.memset("dve", self.biasb[64:128, :, 0:64], NEG, writes=bk)
        qk_ = [("q", c) for c in range(NCH)]
        self.memset("dve", self.qT[64:128, :, 0, :], 0.0, writes=qk_)
        self.memset("dve", self.qT[0:64, :, 1, :], 0.0, writes=qk_)

    def weight_sequence(self):
        cfg = self.cfg
        seq = []
        NT = cfg["SEQ"] // 512

        def mixer_tiles(L):
            if L % 2 == 0:
                return [(L, n) for n in ["q0", "q1", "k0", "k1", "v0", "v1", "o0", "o1"]]
            return [(L, n) for n in ["p0", "p1", "p2", "p3", "w0", "w1"]]

        def ffn_tiles(L):
            return [(L, "i%d" % i) for i in range(11)] + [(L, "f%d" % i) for i in range(8)]

        PH = cfg.get("PH", ("attn", "conv", "ffn"))
        for s in range(cfg["NBP"]):
            for L in range(cfg["DEPTH"]):
                if L % 2 == 0 and "attn" in PH:
                    qkv = [(L, n) for n in ["q0", "q1", "k0", "k1", "v0", "v1"]]
                    wo = [(L, "o0"), (L, "o1")]
                    seq += qkv
                    for t in range(NT):
                        if t + 1 < NT:
                            seq += qkv
                        seq += wo
                if L % 2 == 1 and "conv" in PH:
                    pw1 = [(L, "p%d" % i) for i in range(4)]
                    pw2 = [(L, "w0"), (L, "w1")]
                    seq += pw1
                    for t in range(1, NT):
                        seq += pw1 + pw2
                    seq += pw2
                if "ffn" in PH:
                    for blk in range(NT // 2):
                        seq += ffn_tiles(L)
        if cfg["SAMPLE"]:
            for L in range(cfg["DEPTH"]):
                if ("attn" if L % 2 == 0 else "conv") in PH:
                    seq += mixer_tiles(L)
                if "ffn" in PH:
                    seq += ffn_tiles(L)
        return seq

    def emit(self):
        cfg = self.cfg
        nc = self.nc
        NT = cfg["SEQ"] // 512
        self.plan_weights()
        self.wseq = self.weight_sequence()
        self.wpos = 0
        self.wissued = 0
        cw = ["const"]
        self.dma("sp", self.vecs[:], self.vecs_d[:, :], reads=[], writes=cw)
        self.dma("sp", self.chb[:], self.chb_d[:, :, :], reads=[], writes=cw)
        self.dma("sp", self.identf[:], self.ident_d[:, :], reads=[], writes=cw)
        self.emit_casts()
        self.cp("dve", self.identb[:], self.identf[:], reads=cw, writes=cw)
        self.memset("dve", self.meanmat[:], 1.0 / 1024.0, writes=cw)
        self.memset("dve", self.blkmean[:], 0.0, writes=cw)
        self.memset("dve", self.blkmean[0:64, 0:64], 1.0 / 64.0, writes=cw)
        self.memset("dve", self.blkmean[64:128, 64:128], 1.0 / 64.0, writes=cw)
        self.memset("dve", self.epsc[:], EPS, writes=cw)
        self.memset("dve", self.maskb[:], 0.0, writes=cw)
        self.memset("dve", self.maskb[0:64, :], NEG, writes=cw)
        self.ts("dve", self.gq8[:], self.vecs[:, V_GQ:V_GQ + 2], 0.125, None, ALU.mult, None, reads=cw, writes=cw)

        def load_bias(a):
            self.load_bias(a, [("V", vs, hv) for vs in range(8) for hv in range(2)])
            self.memset("dve", self.V[:, :, :, 64:128], 1.0, writes=[("V", vs, hv) for vs in range(8) for hv in range(2)])

        PH = cfg.get("PH", ("attn", "conv", "ffn"))
        self.PH = PH
        for s in range(cfg["NBP"]):
            if s == 0 or "ffn" not in PH:
                self.load_x_seq(s)
            for L in range(cfg["DEPTH"]):
                blks = []
                for blk in range(NT // 2):
                    tiles = []
                    for t in (2 * blk, 2 * blk + 1):
                        cols = slice(t * 512, (t + 1) * 512)
                        tiles.append((self.xT[:, :, cols], [("x", j, t) for j in range(NCH)],
                                      lambda c, cols=cols: self.xT[:, c, cols]))
                    blks.append(tiles)
                hoisted = False
                if L % 2 == 0 and "attn" in PH:
                    load_bias(L // 2)
                if L % 2 == 1 and "conv" in PH:
                    gens = []
                    for t in range(NT):
                        cols = slice(t * 512, (t + 1) * 512)
                        gens.append(self.conv_tile(L, s, t, self.xT[:, :, cols], [("x", j, t) for j in range(NCH)],
                                                   lambda c, cols=cols: self.xT[:, c, cols]))
                    next(gens[0]); next(gens[0])
                    for t in range(NT):
                        if t + 1 < NT:
                            next(gens[t + 1])
                        next(gens[t]); next(gens[t])
                        if t + 1 < NT:
                            next(gens[t + 1])
                        elif "ffn" in PH:
                            self.ffn_norm(L, blks[0], 512,
                                          h_alias=[("u", j) for j in range(NCH)] + [("dg", 0), ("dg", 1)],
                                          sq_alias=[("sig", 0), ("sig", 1)] + [("uf", j) for j in range(NCH)])
                            hoisted = True
                        next(gens[t])
                for t in range(NT):
                    if ("attn" if L % 2 == 0 else "conv") not in PH or L % 2 == 1:
                        continue
                    cols = slice(t * 512, (t + 1) * 512)
                    xap = self.xT[:, :, cols]
                    xkeys = [("x", j, t) for j in range(NCH)]
                    xsel = lambda c, cols=cols: self.xT[:, c, cols]
                    pass
                if L % 2 == 0 and "attn" in PH:
                    oTk = [("oT", j) for j in range(NCH)] + [("xin", 0), ("xin", 1)]

                    def a_norm(t):
                        cols = slice(t * 512, (t + 1) * 512)
                        self.rmsnorm(self.xT[:, :, cols], [("x", j, t) for j in range(NCH)], V_NMIX + L * 8, 512, self.h,
                                     [("h", j) for j in range(NCH)], slice(0, 512), self.sqA, self.sqAk)

                    def a_out(t):
                        cols = slice(t * 512, (t + 1) * 512)
                        self.out_proj(L, t, ["o0", "o1"], lambda c, cols=cols: self.xT[:, c, cols],
                                      [("x", j, t) for j in range(NCH)], 512, src=self.oT, srck=oTk)

                    a_norm(0)
                    self.qkv_tile(L, s, 0)
                    for t in range(NT):
                        if t + 1 < NT:
                            a_norm(t + 1)
                        self.attn_tile(L, t)
                        if t + 1 < NT:
                            self.qkv_tile(L, s, t + 1)
                        elif "ffn" in PH:
                            self.ffn_norm(L, blks[0], 512,
                                          h_alias=[("q", c) for c in range(NCH)],
                                          sq_alias=[("biasn", 0), ("biasn", 1)] + [("PT", i) for i in range(4)]
                                          + [("rec", 0, 0), ("rec", 0, 1), ("sqh", 0), ("sqh", 1)])
                            hoisted = True
                        a_out(t)
                self.S.barrier()
                if "ffn" in PH:
                    if not hoisted:
                        self.ffn_norm(L, blks[0], 512)
                    for bi, tiles in enumerate(blks):
                        self.ffn_p1(L, tiles, 512)
                        if bi + 1 < len(blks):
                            self.ffn_norm(L, blks[bi + 1], 512)
                        self.ffn_p2(L, tiles, 512)
                        if L == cfg["DEPTH"] - 1:
                            tts = range(8 * bi, 8 * bi + 8)
                            self.store_y_seq(s, tts)
                            if s + 1 < cfg["NBP"]:
                                self.load_x_seq(s + 1, tts)
                self.S.barrier()
            if "ffn" not in PH:
                self.store_y_seq(s)
        if cfg["SAMPLE"]:
            self.S.barrier()
            self.sample_block()
        assert self.wpos == len(self.wseq)
        fin = []
        for i in range(NDMASEM):
            if self.S.dcnt[i] > 0:
                fin.append((("d", i), self.S.dcnt[i]))
        self.S.ops["sp"].append((fin, None, None, 0))

    def sample_block(self):
        cfg = self.cfg
        self.sjobs = [(L, bb) for L in range(0, cfg["DEPTH"], 2) for bb in range(4)]
        xs = self.xsT
        xkeys = [("xs", j) for j in range(NCH)]
        xsel = lambda c: xs[:, c, :]
        self.dma("sp", self.xin[:, 0, :], self.xs_d[:, :], reads=[], writes=[("xin", 0)])
        for g4 in range(2):
            b = self.bank("mm")
            for q in range(4):
                j = g4 * 4 + q
                self.tr(self.ps[:, b, q * 128:(q + 1) * 128], self.xin[:, 0, j * 128:(j + 1) * 128], self.identf[:],
                        reads=[("xin", 0), "const"], writes=[("ps", b)], inc=(q == 3))
            self.cp("dve", xs[:, g4 * 4:(g4 + 1) * 4, :], self.ps[:, b, :].rearrange("p (q n) -> p q n", q=4),
                    reads=[("ps", b)], writes=[("xs", j) for j in range(g4 * 4, g4 * 4 + 4)])
        for L in range(cfg["DEPTH"]):
            if ("attn" if L % 2 == 0 else "conv") not in self.PH:
                pass
            elif L % 2 == 0:
                a = L // 2
                self.load_bias(a, ["Vc", ("Vs", 0), ("Vs", 1)])
                self.memset("dve", self.Vs[:, :, :, 64:128], 1.0, writes=[("Vs", 0), ("Vs", 1)])
                self.memset("dve", self.Vc[:, :, :, 64:128], 1.0, writes=["Vc"])
                self.rmsnorm(xs[:, :, :], xkeys, V_NMIX + L * 8, 128, self.h, [("h", j) for j in range(NCH)], slice(0, 128),
                             self.sqA, self.sqAk)
                self.qkv_tile(L, 0, 0, sample=True)
                self.attn_sample(L)
                self.out_proj(L, 0, ["o0", "o1"], xsel, xkeys, 128, src=self.oT, srck=[("oT", j) for j in range(NCH)])
            else:
                c = L // 2
                for bb in range(4):
                    xi = self.rot("xin", 2)
                    self.dma("sp", self.xin[0:30, xi, :], self.sc[c, bb], reads=[], writes=[("xin", xi)])
                    for g4 in range(2):
                        b = self.bank("mm")
                        for q in range(4):
                            j = g4 * 4 + q
                            self.tr(self.ps[:, b, q * 128:q * 128 + 30], self.xin[0:30, xi, j * 128:(j + 1) * 128],
                                    self.identf[0:30, 0:30], reads=[("xin", xi), "const"], writes=[("ps", b)], inc=(q == 3))
                        dst = self.u[:, g4 * 4:g4 * 4 + 4, bb * 62:bb * 62 + 30]
                        src = self.ps[:, b, :].rearrange("p (q n) -> p q n", q=4)[:, :, 0:30]
                        self.cp("dve", dst, src, reads=[("ps", b)], writes=[("u", j) for j in range(g4 * 4, g4 * 4 + 4)])
                for _ in self.conv_tile(L, 0, 0, xs[:, :, :], xkeys, xsel, sample=True):
                    pass
            self.S.barrier()
            if "ffn" in self.PH:
                self.ffn(L, [(xs[:, :, :], xkeys, xsel)], 128)
            self.S.barrier()
        for g4 in range(2):
            b = self.bank("mm")
            for q in range(4):
                j = g4 * 4 + q
                self.tr(self.ps[:, b, q * 128:(q + 1) * 128], xs[:, j, :], self.identf[:],
                        reads=[("xs", j), "const"], writes=[("ps", b)], inc=(q == 3))
            oi = self.rot("ost", 3)
            self.cp("dve", self.ost[:, oi, :], self.ps[:, b, :], reads=[("ps", b)], writes=[("ost", oi)])
            self.dma("sp", self.ys_d[:, g4 * 512:(g4 + 1) * 512], self.ost[:, oi, :], reads=[("ost", oi)], writes=[])


def build(cfg):
    nc = bass.Bass("TRN2", target_bir_lowering=False)
    g = Gen(nc, cfg)
    NBP, SEQ, DEPTH = cfg["NBP"], cfg["SEQ"], cfg["DEPTH"]
    NA = (DEPTH + 1) // 2
    NC_ = DEPTH // 2

    def din(name, shape, dt=F32):
        return nc.dram_tensor(name, list(shape), dt, kind="ExternalInput").ap()

    def dout(name, shape, dt=F32):
        return nc.dram_tensor(name, list(shape), dt, kind="ExternalOutput").ap()

    g.xp = din("xp", [NBP, SEQ, D])
    g.xs_d = din("xs", [128, D])
    g.ck = din("ck", [NA, 4, 512, D])
    g.cv = din("cv", [NA, 4, 512, D])
    g.sc = din("sc", [max(NC_, 1), 4, 30, D])
    g.W = {
        "w_qkv": din("w_qkv", [NA, D, 3 * D]),
        "w_o": din("w_o", [NA, D, D]),
        "pw1_w": din("pw1_w", [max(NC_, 1), D, 2 * D]),
        "pw2_w": din("pw2_w", [max(NC_, 1), D, D]),
        "ffn_w_in": din("ffn_w_in", [DEPTH, D, 2 * DFF]),
        "ffn_w_out": din("ffn_w_out", [DEPTH, DFF, D]),
    }
    g.vecs_d = din("vecs", [128, NV])
    g.chb_d = din("chb", [128, 2, NH])
    g.biasn_d = din("biasn", [2, 128, NH, 256])
    g.ident_d = din("ident", [128, 128])
    g.yp = dout("yp", [NBP, SEQ, D])
    g.ys_d = dout("ys", [128, D])
    g.kp = dout("kp", [NA, NBP, 512, D])
    g.vp = dout("vp", [NA, NBP, 512, D])
    g.ksn = dout("ksn", [NA, 128, D])
    g.vsn = dout("vsn", [NA, 128, D])
    g.cpn = dout("cpn", [max(NC_, 1), NBP, 30, D])
    g.csn = dout("csn", [max(NC_, 1), 4, 30, D])
    g.plan_weights()
    g.wscr = nc.dram_tensor("wscr", [g.ntiles, 128, 4096], BF16).ap()

    with contextlib.ExitStack() as st:
        def sb(name, shape, dt):
            return st.enter_context(nc.sbuf_tensor(name, list(shape), dt))

        g.xT = sb("xT", [128, NCH, SEQ], F32)
        g.h = sb("h", [128, NCH, 512], BF16)
        g.wslots = sb("wslots", [128, NSLOT, 4096], BF16)
        g.ost = sb("ost", [128, 3, 512], F32)
        g.vecs = sb("vecs_sb", [128, NV], F32)
        g.chb = sb("chb_sb", [128, 2, NH], F32)
        g.identf = sb("identf", [128, 128], F32)
        g.identb = sb("identb", [128, 128], BF16)
        g.meanmat = sb("meanmat", [128, 128], BF16)
        g.blkmean = sb("blkmean", [128, 128], BF16)
        g.gq8 = sb("gq8", [128, 2], F32)
        g.epsc = sb("epsc", [128, 1], F32)
        g.maskb = sb("maskb", [128, 64], BF16)
        g.rstd = sb("rstd", [128, 512], F32)
        ASZ = 86 * 1024
        arena = sb("arena", [128, ASZ], mybir.dt.uint8)
        off = [0]

        def carve(shape, dt, reset=None):
            if reset is not None:
                off[0] = reset
            n = int(np.prod(shape[1:])) * (4 if dt == F32 else 2)
            n = (n + 63) // 64 * 64
            ap = arena[:, off[0]:off[0] + n].bitcast(dt)
            off[0] += n
            assert off[0] <= ASZ, (off[0], ASZ)
            if len(shape) == 2:
                return ap
            if len(shape) == 3:
                return ap.rearrange("p (a b) -> p a b", a=shape[1])
            if len(shape) == 4:
                return ap.rearrange("p (a b c) -> p a b c", a=shape[1], b=shape[2])
            raise ValueError

        g.qT = carve([128, NCH, 2, 512], BF16, reset=0)
        kT_off = off[0]
        g.kT = carve([128, NCH, 1024], BF16)
        V_off = off[0]
        g.V = carve([128, 8, 8, 192], BF16)
        g.biasst = carve([128, NH, 256], F32, reset=V_off)
        off[0] = V_off + 24 * 1024
        g.biasb = carve([128, NH, 256], BF16)
        sqA_off = off[0]
        g.PT = carve([128, 4, 512], BF16)
        g.rec = carve([128, 1, 512], F32)
        g.sqh = carve([128, 2, 512], BF16)
        g.sqA = carve([128, NCH, 512], BF16, reset=sqA_off)
        g.sqAk = [("PT", i) for i in range(4)] + [("rec", 0, 0), ("rec", 0, 1), ("sqh", 0), ("sqh", 1)]
        g.rs = carve([128, 2, 512], F32)
        g.kf = carve([128, 1, 512], F32)
        g.oT = carve([128, NCH, 512], BF16)
        xin_off = ASZ - 8 * 1024
        g.u = carve([128, NCH, 544], BF16, reset=0)
        g.dg = carve([128, 2, CW, 128], BF16)
        g.y = carve([128, NCH, 512], F32)
        g.ybf = carve([128, NCH, 512], BF16)
        g.ysq = carve([128, NCH, 512], BF16)
        g.st = carve([128, 3, 512], F32)
        g.sig = carve([128, 2, 512], F32)
        g.uf = carve([128, NCH, 128], F32)
        assert off[0] <= xin_off, (off[0], xin_off)
        g.h2 = carve([128, NCH, 1024], BF16, reset=0)
        g.aT = carve([128, NFF, 1024], BF16)
        g.sg = carve([128, 3, 512], BF16)
        g.sqF = carve([128, NCH, 512], BF16)
        assert off[0] <= xin_off, (off[0], xin_off)
        g.xin = carve([128, 2, 1024], F32, reset=xin_off)
        if cfg["SAMPLE"]:
            g.xsT = sb("xsT", [128, NCH, 128], F32)
            g.kc = carve([128, NCH, 512], BF16, reset=kT_off)
            xv = g.xT[:].rearrange("p a b -> p (a b)")
            g.ck32 = [xv[:, (2 * i) * 4096:(2 * i + 1) * 4096].rearrange("p (m n) -> p m n", m=4) for i in range(2)]
            g.cv32 = [xv[:, (2 * i + 1) * 4096:(2 * i + 2) * 4096].rearrange("p (m n) -> p m n", m=4) for i in range(2)]
            g.Vc = carve([128, 4, 8, 192], BF16, reset=V_off)
            g.Vs = carve([128, 4, 8, 192], BF16)
            assert off[0] <= V_off + 24 * 1024
            g.kTs = g.qT[:, :, 0, 128:256]
        g.ps_t = st.enter_context(nc.psum_tensor("ps", [128, 8, 512], F32))
        g.ps = g.ps_t

        g.emit()
        sems = {}
        for e in ENGS:
            sems[e] = st.enter_context(nc.semaphore("sem_" + e))
        for i in range(NDMASEM):
            sems[("d", i)] = st.enter_context(nc.semaphore("sem_d%d" % i))

        def replay(eng, e):
            for waits, fn, incsem, incv in g.S.ops[eng]:
                for s, v in waits:
                    e.wait_ge(sems[s], v)
                if fn is None:
                    continue
                ins = fn(e)
                if incsem is not None:
                    ins.then_inc(sems[incsem], incv)

        with nc.Block() as block:
            @block.sync
            def _(e):
                replay("sp", e)

            @block.tensor
            def _(e):
                replay("pe", e)

            @block.scalar
            def _(e):
                replay("act", e)

            @block.vector
            def _(e):
                replay("dve", e)

            @block.gpsimd
            def _(e):
                replay("pool", e)
    return nc, g


def _host_params(inp):
    vecs = np.zeros((128, NV), np.float32)

    def put(col, arr):
        a = np.asarray(arr, np.float32)
        n = a.shape[-1] // 128
        a2 = a.reshape(-1, n, 128)
        a2 = np.transpose(a2, (2, 0, 1)).reshape(128, -1)
        vecs[:, col:col + a2.shape[1]] = a2

    put(V_NMIX, inp["norm_mix"])
    put(V_NFFN, inp["norm_ffn"])
    put(V_PW1B, inp["pw1_b"])
    put(V_DWB, inp["dw_b"])
    put(V_LNG, inp["conv_ln_g"])
    put(V_LNB, inp["conv_ln_b"])
    put(V_PW2B, inp["pw2_b"])
    vecs[:, V_GQ:V_GQ + 2] = np.tile(np.asarray(inp["q_norm"], np.float32).T, (2, 1))
    vecs[:, V_GK:V_GK + 2] = np.tile(np.asarray(inp["k_norm"], np.float32).T, (2, 1))
    dw = np.asarray(inp["dw_w"], np.float32)
    dw = dw.reshape(2, CW, 8, 128)
    vecs[:, V_DWW:] = np.transpose(dw, (3, 0, 2, 1)).reshape(128, -1)
    tab = np.asarray(inp["rel_table"], np.float32)
    chb = np.ascontiguousarray(np.broadcast_to(tab[:, :, 256][None], (128, 2, NH))).astype(np.float32)
    p = np.arange(128)[:, None]
    col = np.arange(256)[None, :]
    idx = np.clip(col - p, -128, 128) + 128
    biasn = np.ascontiguousarray(np.transpose(tab[:, :, idx], (0, 2, 1, 3))).astype(np.float32)
    return vecs, chb, biasn


def kernel(**inp):
    ncores = 8
    cfg = {"NBP": 4, "SEQ": 2048, "DEPTH": 4, "SAMPLE": True}
    nc, g = build(cfg)
    vecs, chb, biasn = _host_params(inp)
    ident = np.eye(128, dtype=np.float32)
    f = lambda k: np.ascontiguousarray(np.asarray(inp[k], np.float32))
    xp, xs = f("x_prompt"), f("x_sample")
    ck, cv, sc = f("cache_k"), f("cache_v"), f("state_conv")
    shared = {k: f(k) for k in ("w_qkv", "w_o", "pw1_w", "pw2_w", "ffn_w_in", "ffn_w_out")}
    shared.update({"vecs": vecs, "chb": chb, "biasn": biasn, "ident": ident})
    in_maps = []
    for c in range(ncores):
        b0, b1 = 4 * c, 4 * c + 4
        m = dict(shared)
        m["xp"] = xp[b0:b1]
        m["xs"] = xs[b0:b1].reshape(128, D)
        m["ck"] = ck[:, b0:b1].reshape(2, 4, 512, D)
        m["cv"] = cv[:, b0:b1].reshape(2, 4, 512, D)
        m["sc"] = sc[:, b0:b1]
        in_maps.append(m)
    res = run_bass_kernel_spmd(nc, in_maps, core_ids=list(range(ncores)))
    R = res.results
    yp = np.concatenate([r["yp"] for r in R], axis=0)
    ys = np.concatenate([r["ys"].reshape(4, 32, D) for r in R], axis=0)
    kp = np.concatenate([r["kp"] for r in R], axis=1).reshape(2, 32, 512, NH, 64)
    vp = np.concatenate([r["vp"] for r in R], axis=1).reshape(2, 32, 512, NH, 64)
    ksn = np.concatenate([r["ksn"].reshape(2, 4, 32, NH, 64) for r in R], axis=1)
    vsn = np.concatenate([r["vsn"].reshape(2, 4, 32, NH, 64) for r in R], axis=1)
    cpn = np.concatenate([r["cpn"] for r in R], axis=1)
    csn = np.concatenate([r["csn"] for r in R], axis=1)
    return (yp.astype(np.float32), ys.astype(np.float32), kp.astype(np.float32), vp.astype(np.float32),
            ksn.astype(np.float32), vsn.astype(np.float32), cpn.astype(np.float32), csn.astype(np.float32))
```
